# Optimizing a Trainium2 kernel written in Bass

```python
import jax, jax.numpy as jnp
from jax import lax
import numpy as np

D_MODEL = 1024
BATCH = 32
SEQ = 2048
DEPTH = 1

HEAD_DIM = 64
N_Q_HEADS = 8
N_KV_HEADS = 2
Q_PER_KV = N_Q_HEADS // N_KV_HEADS
ATTN_WIDTH = N_Q_HEADS * HEAD_DIM
KV_WIDTH = N_KV_HEADS * HEAD_DIM
WINDOW = 128
BLOCK = WINDOW
ROPE_THETA = 10000.0
POOL_WINDOWS = (2, 4, 8, 16)
N_POOL_GROUPS = len(POOL_WINDOWS)
POOL_WIDTH = D_MODEL - ATTN_WIDTH
POOL_GROUP_DIM = POOL_WIDTH // N_POOL_GROUPS
MIX_WIDTH = ATTN_WIDTH + POOL_WIDTH
IN_WIDTH = ATTN_WIDTH + 2 * KV_WIDTH + POOL_WIDTH
D_FF = 4 * D_MODEL
EPS = 1e-6

kernel_name = "hymba_swa_sink_multiscale_pool_block"


def _rmsnorm(x, g):
    xf = x.astype(jnp.float32)
    y = xf * lax.rsqrt(jnp.mean(xf * xf, axis=-1, keepdims=True) + EPS)
    return (y * g.astype(jnp.float32)).astype(x.dtype)


def _rope(x, pos):
    half = HEAD_DIM // 2
    inv_freq = ROPE_THETA ** (-jnp.arange(half, dtype=jnp.float32) / half)
    ang = pos.astype(jnp.float32)[:, None] * inv_freq[None, :]
    cos = jnp.cos(ang)[None, :, None, :]
    sin = jnp.sin(ang)[None, :, None, :]
    xf = x.astype(jnp.float32)
    x1, x2 = xf[..., :half], xf[..., half:]
    out = jnp.concatenate([x1 * cos - x2 * sin, x2 * cos + x1 * sin], axis=-1)
    return out.astype(x.dtype)


def _swa_with_sinks(q, k, v, sinks):
    B, S = q.shape[0], q.shape[1]
    nb = S // BLOCK
    qb = q.reshape(B, nb, BLOCK, N_KV_HEADS, Q_PER_KV, HEAD_DIM)
    kb = k.reshape(B, nb, BLOCK, N_KV_HEADS, HEAD_DIM)
    vb = v.reshape(B, nb, BLOCK, N_KV_HEADS, HEAD_DIM)

    def with_prev(t):
        prev = jnp.pad(t[:, :-1], ((0, 0), (1, 0), (0, 0), (0, 0), (0, 0)))
        return jnp.concatenate([prev, t], axis=2)

    kw, vw = with_prev(kb), with_prev(vb)
    scale = HEAD_DIM ** -0.5
    logits = jnp.einsum('bnqkgd,bnskd->bnkgqs', qb, kw,
                        preferred_element_type=jnp.float32) * scale
    blk = jnp.arange(nb)[:, None, None]
    qi = jnp.arange(BLOCK)[None, :, None]
    kj = jnp.arange(2 * BLOCK)[None, None, :]
    rel = BLOCK + qi - kj
    kpos = (blk - 1) * BLOCK + kj
    mask = (rel >= 0) & (rel < WINDOW) & (kpos >= 0)
    logits = jnp.where(mask[None, :, None, None], logits, -jnp.inf)
    sink = sinks.astype(jnp.float32).reshape(N_KV_HEADS, Q_PER_KV)[None, None, :, :, None, None]
    m = jnp.maximum(jnp.max(logits, axis=-1, keepdims=True), sink)
    p = jnp.exp(logits - m)
    denom = jnp.sum(p, axis=-1, keepdims=True) + jnp.exp(sink - m)
    probs = (p / denom).astype(v.dtype)
    out = jnp.einsum('bnkgqs,bnskd->bnqkgd', probs, vw)
    return out.reshape(B, S, ATTN_WIDTH)


def _multiscale_pool(u, w_pool, pool_scale):
    B, S = u.shape[0], u.shape[1]
    ug = u.reshape(B, S, N_POOL_GROUPS, POOL_GROUP_DIM).astype(jnp.float32)
    c = jnp.pad(jnp.cumsum(ug, axis=1), ((0, 0), (1, 0), (0, 0), (0, 0)))
    t = jnp.arange(S)
    means = []
    for g, w in enumerate(POOL_WINDOWS):
        cg = c[:, :, g]
        lagged = jnp.pad(cg[:, :S + 1 - w], ((0, 0), (w - 1, 0), (0, 0)))
        cnt = jnp.minimum(t + 1, w).astype(jnp.float32)[None, :, None]
        means.append((cg[:, 1:] - lagged) / cnt)
    mean = jnp.stack(means, axis=2)
    d = (mean - ug).astype(u.dtype)
    y = jnp.einsum('bsgc,gcd->bsgd', d, w_pool)
    return y.reshape(B, S, POOL_WIDTH) * pool_scale


def setup_inputs(seed: int = 0) -> dict:
    key = jax.random.key(seed)
    ks = jax.random.split(key, 12)
    f32 = jnp.float32
    x = jax.random.normal(ks[0], (BATCH, SEQ, D_MODEL), f32)
    attn_norm_g = 1.0 + 0.02 * jax.random.normal(ks[1], (DEPTH, D_MODEL), f32)
    w_in = jax.random.normal(ks[2], (DEPTH, D_MODEL, IN_WIDTH), f32) * D_MODEL ** -0.5
    attn_sinks = 0.5 * jax.random.normal(ks[3], (DEPTH, N_Q_HEADS), f32)
    w_pool = jax.random.normal(ks[4], (DEPTH, N_POOL_GROUPS, POOL_GROUP_DIM, POOL_GROUP_DIM), f32) * POOL_GROUP_DIM ** -0.5
    pool_scale = 1.0 + 0.1 * jax.random.normal(ks[5], (DEPTH, POOL_WIDTH), f32)
    w_out = jax.random.normal(ks[6], (DEPTH, MIX_WIDTH, D_MODEL), f32) * MIX_WIDTH ** -0.5
    mlp_norm_g = 1.0 + 0.02 * jax.random.normal(ks[7], (DEPTH, D_MODEL), f32)
    w_up = jax.random.normal(ks[8], (DEPTH, D_MODEL, D_FF), f32) * D_MODEL ** -0.5
    w_down = jax.random.normal(ks[9], (DEPTH, D_FF, D_MODEL), f32) * D_FF ** -0.5
    final_norm_g = 1.0 + 0.02 * jax.random.normal(ks[10], (D_MODEL,), f32)
    return {"x": x, "attn_norm_g": attn_norm_g, "w_in": w_in, "attn_sinks": attn_sinks,
            "w_pool": w_pool, "pool_scale": pool_scale, "w_out": w_out,
            "mlp_norm_g": mlp_norm_g, "w_up": w_up, "w_down": w_down,
            "final_norm_g": final_norm_g}


def reference(x, attn_norm_g, w_in, attn_sinks, w_pool, pool_scale, w_out,
              mlp_norm_g, w_up, w_down, final_norm_g):
    B, S = x.shape[0], x.shape[1]
    pos = jnp.arange(S)
    for l in range(DEPTH):
        h = _rmsnorm(x, attn_norm_g[l])
        proj = h @ w_in[l]
        q = proj[..., :ATTN_WIDTH].reshape(B, S, N_Q_HEADS, HEAD_DIM)
        k = proj[..., ATTN_WIDTH:ATTN_WIDTH + KV_WIDTH].reshape(B, S, N_KV_HEADS, HEAD_DIM)
        v = proj[..., ATTN_WIDTH + KV_WIDTH:ATTN_WIDTH + 2 * KV_WIDTH].reshape(B, S, N_KV_HEADS, HEAD_DIM)
        u = proj[..., ATTN_WIDTH + 2 * KV_WIDTH:]
        q, k = _rope(q, pos), _rope(k, pos)
        attn = _swa_with_sinks(q, k, v, attn_sinks[l])
        pool = _multiscale_pool(u, w_pool[l], pool_scale[l])
        x = x + jnp.concatenate([attn, pool], axis=-1) @ w_out[l]
        h = _rmsnorm(x, mlp_norm_g[l])
        x = x + jnp.square(jax.nn.relu(h @ w_up[l])) @ w_down[l]
    return _rmsnorm(x, final_norm_g)
```

```python
import numpy as np
import concourse.bass as bass
import concourse.mybir as mybir
from concourse.bass_utils import run_bass_kernel_spmd

F32 = mybir.dt.float32
BF16 = mybir.dt.bfloat16
ALU = mybir.AluOpType
AF = mybir.ActivationFunctionType

NCORES = 8
D = 1024
SEQ = 2048
T = 512
NB = 4
SEQ_PER_CORE = 4
TOK = SEQ_PER_CORE * SEQ
NT = TOK // T
TPS = SEQ // T
NCOL = 1408
DFF = 4096
EPS = 1e-6
POOL_W = (2, 4, 8, 16)
NPIECE = 8
NSLOT = 3
DBG_STOP = None
DBG_OPLIMIT = None
DBG_PRINT = False
DO_SCHEDULE = True

QC0 = 0
KC0 = 512
UC0 = 768
VC0 = 1280


class _Op:
    __slots__ = ("eng", "fn", "deps", "lane", "idx", "sig", "waits", "is_dma", "cost", "nbytes", "tag", "est", "prio")

    def __init__(self, eng, fn, lane):
        self.eng = eng
        self.fn = fn
        self.lane = lane
        self.is_dma = lane is not None
        self.deps = set()
        self.sig = None
        self.waits = []


class Prog:
    ENGS = ("sp", "act", "pool", "dve", "pe")

    def __init__(self):
        self.ops = []
        self.last_writer = {}
        self.readers = {}
        self.lane_counts = {}
        self.lane_waitall = set()
        self.cur_tag = None

    PER = {"act": (130.0, 1.25), "dve": (200.0, 1.3), "pool": (150.0, 2.5), "pe": (6.0, 0.42), "sp": (100.0, 0.0)}

    def op(self, eng, fn, reads=(), writes=(), lane=None, waitall=False, n=0, k=1, nbytes=0):
        if DBG_OPLIMIT is not None and len(self.ops) >= DBG_OPLIMIT:
            return None
        if DBG_PRINT:
            print("OP", len(self.ops), eng, "R", list(reads)[:3], "W", list(writes)[:3], flush=True)
        writes = list(writes) + [r for r in reads if r[0] == "ps" and r not in writes]
        reads = [r for r in reads if r[0] != "ps"]
        o = _Op(eng, fn, lane)
        o.idx = len(self.ops)
        fx, pe_ = self.PER[eng]
        o.cost = k * fx + n * pe_
        o.nbytes = nbytes
        o.tag = self.cur_tag
        o.prio = 1 if (self.cur_tag is not None and self.cur_tag[1] in ('F', 'G', 'H')) else 0
        o.est = 0.0
        for r in reads:
            w = self.last_writer.get(r)
            if w is not None:
                o.deps.add(w)
        for r in writes:
            w = self.last_writer.get(r)
            if w is not None:
                o.deps.add(w)
            for rd in self.readers.get(r, ()):
                o.deps.add(rd)
        for r in writes:
            self.last_writer[r] = o.idx
            self.readers[r] = []
        for r in reads:
            if r not in writes:
                self.readers.setdefault(r, []).append(o.idx)
        o.deps.discard(o.idx)
        if lane is not None and waitall:
            o.deps = {d for d in o.deps if self.ops[d].lane != lane}
        if lane is not None:
            self.lane_counts[lane] = self.lane_counts.get(lane, 0) + 1
            o.sig = ("lane:" + lane, 16 * self.lane_counts[lane])
            if waitall:
                self.lane_waitall.add(lane)
        self.ops.append(o)
        return o

    def schedule(self):
        import heapq
        ops = self.ops
        n = len(ops)
        succ = [[] for _ in range(n)]
        indeg = [0] * n
        for o in ops:
            for d in o.deps:
                succ[d].append(o.idx)
                indeg[o.idx] += 1
        ready = [0.0] * n
        pending = {e: [] for e in self.ENGS}
        avail = {e: [] for e in self.ENGS}
        for o in ops:
            if indeg[o.idx] == 0:
                heapq.heappush(pending[o.eng], (0.0, o.idx))
        free = {e: 0.0 for e in self.ENGS}
        dma_free = 0.0
        order = []
        while len(order) < n:
            best = None
            for e in self.ENGS:
                pe_, av = pending[e], avail[e]
                while pe_ and pe_[0][0] <= free[e]:
                    i_ = heapq.heappop(pe_)[1]
                    heapq.heappush(av, (ops[i_].prio, i_))
                if av:
                    cand = (free[e], av[0][1], e, True)
                elif pe_:
                    cand = (pe_[0][0], pe_[0][1], e, False)
                else:
                    continue
                if best is None or cand[:2] < best[:2]:
                    best = cand
            start, idx, e, from_av = best
            if from_av:
                heapq.heappop(avail[e])
            else:
                heapq.heappop(pending[e])
            o = ops[idx]
            if o.is_dma:
                free[e] = start + (1000.0 if e == "pool" else 100.0)
                d0 = max(start, dma_free)
                dma_free = d0 + o.nbytes / 140.0
                fin = dma_free + 1800.0
            else:
                free[e] = start + o.cost
                fin = free[e]
            order.append(idx)
            o.est = start
            for sidx in succ[idx]:
                so = ops[sidx]
                lat = 80.0 if (so.eng == e and not o.is_dma) else 500.0
                if fin + lat > ready[sidx]:
                    ready[sidx] = fin + lat
                indeg[sidx] -= 1
                if indeg[sidx] == 0:
                    heapq.heappush(pending[so.eng], (ready[sidx], sidx))
        self.est_total = max(free.values())
        newidx = {old: new for new, old in enumerate(order)}
        new_ops = [ops[i] for i in order]
        for o in new_ops:
            o.deps = {newidx[d] for d in o.deps}
            o.idx = newidx[o.idx]
        self.ops = new_ops

    def finalize(self):
        ops = self.ops
        needed = set()
        for o in ops:
            for d in o.deps:
                do = ops[d]
                if do.is_dma:
                    continue
                if do.eng == "pe" and o.eng == "pe" and not o.is_dma:
                    continue
                needed.add(d)
        cnt = {e: 0 for e in self.ENGS}
        for o in ops:
            if o.is_dma:
                continue
            if o.idx in needed:
                cnt[o.eng] += 1
                o.sig = ("eng:" + o.eng, cnt[o.eng])
        seen = {e: {} for e in self.ENGS}
        for o in ops:
            req = {}
            for d in o.deps:
                do = ops[d]
                if (not do.is_dma) and do.eng == "pe" and o.eng == "pe" and not o.is_dma:
                    continue
                key, val = do.sig
                if do.is_dma and do.lane in self.lane_waitall:
                    val = 16 * self.lane_counts[do.lane]
                if val > req.get(key, 0):
                    req[key] = val
            s = seen[o.eng]
            for key, val in req.items():
                if s.get(key, 0) >= val:
                    continue
                s[key] = val
                o.waits.append((key, val))

    def emit(self, nc, sems, block, final_waits):
        per = {e: [o for o in self.ops if o.eng == e] for e in self.ENGS}

        def run(e, name):
            for o in per[name]:
                for key, val in o.waits:
                    e.wait_ge(sems[key], val)
                ins = o.fn(e)
                if o.sig is not None:
                    ins.then_inc(sems[o.sig[0]], 16 if o.is_dma else 1)
            if name == "sp":
                for key, val in final_waits:
                    e.wait_ge(sems[key], val)

        @block.sync
        def _(e):
            run(e, "sp")

        @block.scalar
        def _(e):
            run(e, "act")

        @block.gpsimd
        def _(e):
            run(e, "pool")

        @block.vector
        def _(e):
            run(e, "dve")

        @block.tensor
        def _(e):
            run(e, "pe")


def build_nc():
    nc = bass.Bass("TRN2", target_bir_lowering=False)
    P = Prog()

    def din(name, shape, dt=F32):
        return nc.dram_tensor(name, list(shape), dt, kind="ExternalInput").ap()

    xin = din("x", [TOK, D])
    w_in_p = din("w_in_p", [D, NCOL])
    w_out_d = din("w_out", [D, D])
    w_up_d = din("w_up", [D, DFF])
    w_dn_d = din("w_down", [DFF, D])
    w_pool_d = din("w_pool", [512, 128])
    g1_d = din("g1", [1, D])
    g2_d = din("g2", [1, D])
    gf_d = din("gf", [1, D])
    psc_d = din("pscale", [128, 4])
    sink_d = din("sinks", [1, 8])
    cos_d = din("cosT", [128, SEQ])
    sin_d = din("sinT", [128, SEQ])
    rot_d = din("rotm", [128, 128])
    idn_d = din("ident", [128, 128])
    msk_d = din("maskcn", [128, 256])
    icn_d = din("invcnt", [1, 64])
    yout = nc.dram_tensor("y", [TOK, D], F32, kind="ExternalOutput").ap()
    wup_s = nc.dram_tensor("wup_s", [D, DFF], BF16, kind="Internal").ap()
    wdn_s = nc.dram_tensor("wdn_s", [DFF, D], BF16, kind="Internal").ap()

    import contextlib
    es_ = contextlib.ExitStack()
    with es_:
        def sb(name, shape, dt):
            return es_.enter_context(nc.sbuf_tensor(name, list(shape), dt))

        LU = 16 + T
        wcat = sb("wcat", [128, 8, NCOL], BF16)
        wout = sb("wout", [128, 8, D], BF16)
        wpool = sb("wpool", [128, 4, 128], BF16)
        wu = [sb(f"wu{i}", [128, 8, 512], BF16) for i in range(NSLOT)]
        wd = [sb(f"wd{i}", [128, 4, D], BF16) for i in range(NSLOT)]
        xs = [sb(f"xres{i}", [128, NB, D], F32) for i in range(2)]
        hTs = [sb(f"hT{i}", [128, 8, T], BF16) for i in range(2)]
        junkH = sb("junkH", [128, D], BF16)
        hbuf = [sb(f"hbuf{i}", [128, D], BF16) for i in range(2)]
        qT = sb("qT", [128, 4, T], BF16)
        kT = [sb(f"kT{i}", [128, 2, T], BF16) for i in range(2)]
        vaug = [sb(f"vaug{i}", [128, NB, 2, 128], BF16) for i in range(2)]
        ub = [sb(f"ub{i}", [128, LU], F32) for i in range(2)]
        hsave = sb("hsave", [128, 4, 16], F32)
        tmpA = sb("tmpA", [128, LU], F32)
        tmpB = sb("tmpB", [128, LU], F32)
        dT = [sb(f"dT{i}", [128, T], BF16) for i in range(2)]
        yT = sb("yT", [128, 4, T], BF16)
        attnT = sb("attnT", [128, 4, T], BF16)
        PT = [[sb(f"PT{i}{k}", [128, 4, 256], BF16) for k in range(2)] for i in range(2)]
        aT = [sb(f"aT{i}", [128, 8, T], BF16) for i in range(2)]
        cosb = sb("cosb", [128, T], F32)
        sinb = sb("sinb", [128, T], F32)
        g1b = sb("g1b", [128, D], BF16)
        g2b = sb("g2b", [128, D], BF16)
        gfb = sb("gfb", [128, D], BF16)
        maskt = sb("maskt", [128, 256], BF16)
        rotm = sb("rotm_s", [128, 128], BF16)
        ident = sb("ident_s", [128, 128], BF16)
        sinkt = sb("sinkt", [128, 8], F32)
        est = sb("est", [128, 8], F32)
        psct = sb("psct", [128, 4], F32)
        icnt = sb("icnt", [128, 64], F32)
        st = sb("stats", [128, 48], F32)
        fix = sb("fixt", [128, 16], F32)
        xb = dT[0]
        ropeA, ropeB = tmpA, tmpB
        sA, sB = tmpA, tmpB

        banks = [es_.enter_context(nc.psum_tensor(f"bank{i}", [128, 512], F32)) for i in range(8)]
        UB = (0, 1)
        DB = (2, 3)
        MB = (4, 5, 6, 7)

        PL = "prolog"
        PW = "prolog_w"
        for kc in range(8):
            P.op("pool", lambda e, kc=kc: e.dma_start(out=wcat[:, kc, :], in_=w_in_p[kc * 128:(kc + 1) * 128, :]),
                 writes=[("wcat",)], lane=PL, waitall=True, nbytes=128 * NCOL * 4)
        for kc in range(8):
            P.op("pool", lambda e, kc=kc: e.dma_start(out=wout[:, kc, :], in_=w_out_d[kc * 128:(kc + 1) * 128, :]),
                 writes=[("wout",)], lane=PL, waitall=True, nbytes=128 * D * 4)
        for g in range(4):
            P.op("pool", lambda e, g=g: e.dma_start(out=wpool[:, g, :], in_=w_pool_d[g * 128:(g + 1) * 128, :]),
                 writes=[("wpool",)], lane=PL, waitall=True, nbytes=65536)
        P.op("pool", lambda e: e.dma_start(out=rotm[:], in_=rot_d[:, :]), writes=[("consts",)], lane=PL, waitall=True, nbytes=65536)
        P.op("pool", lambda e: e.dma_start(out=ident[:], in_=idn_d[:, :]), writes=[("consts",)], lane=PL, waitall=True, nbytes=65536)
        P.op("pool", lambda e: e.dma_start(out=maskt[:], in_=msk_d[:, :]), writes=[("consts",)], lane=PL, waitall=True, nbytes=131072)
        P.op("pool", lambda e: e.dma_start(out=g1b[:], in_=g1_d.partition_broadcast(128)), writes=[("g1b",)], lane=PL, waitall=True, nbytes=4096)
        P.op("pool", lambda e: e.dma_start(out=g2b[:], in_=g2_d.partition_broadcast(128)), writes=[("g2b",)], lane=PL, waitall=True, nbytes=4096)
        P.op("pool", lambda e: e.dma_start(out=gfb[:], in_=gf_d.partition_broadcast(128)), writes=[("gfb",)], lane=PL, waitall=True, nbytes=4096)
        for i in range(8):
            P.op("pool", lambda e, i=i: e.dma_start(out=wup_s[i * 128:(i + 1) * 128, :], in_=w_up_d[i * 128:(i + 1) * 128, :]),
                 writes=[("wup_s",)], lane=PW, waitall=True, nbytes=128 * DFF * 4)
        for i in range(8):
            P.op("pool", lambda e, i=i: e.dma_start(out=wdn_s[i * 512:(i + 1) * 512, :], in_=w_dn_d[i * 512:(i + 1) * 512, :]),
                 writes=[("wdn_s",)], lane=PW, waitall=True, nbytes=512 * D * 4)
        PS = "prolog_sp"
        P.op("sp", lambda e: e.dma_start(out=sinkt[:], in_=sink_d.partition_broadcast(128)), writes=[("sinkt",)], lane=PS, waitall=True, nbytes=4096)
        P.op("sp", lambda e: e.dma_start(out=psct[:], in_=psc_d[:, :]), writes=[("psct",)], lane=PS, waitall=True, nbytes=2048)
        P.op("sp", lambda e: e.dma_start(out=icnt[:], in_=icn_d.partition_broadcast(128)), writes=[("icnt",)], lane=PS, waitall=True, nbytes=32768)
        P.op("act", lambda e: e.activation(out=est[:], in_=sinkt[:], func=AF.Exp), reads=[("sinkt",)], writes=[("est",)], n=8)
        for i in range(2):
            P.op("pool", lambda e, i=i: e.memset(vaug[i][:], 1.0), writes=[("vaug", i, b) for b in range(NB)], n=1024)

        def rstd_ops(xt, xk, b, junk, junk_key, sc):
            c_ss, c_ln, c_rs = sc + b, sc + 4 + b, sc + 8 + b
            P.op("act", lambda e: e.activation(out=junk, in_=xt[:, b, :], func=AF.Square, accum_out=st[:, c_ss:c_ss + 1]),
                 reads=[(xk, b)], writes=[junk_key, ("st", c_ss)], n=D)
            P.op("act", lambda e: e.activation(out=st[:, c_ln:c_ln + 1], in_=st[:, c_ss:c_ss + 1], func=AF.Ln,
                                               scale=1.0 / D, bias=EPS),
                 reads=[("st", c_ss)], writes=[("st", c_ln)], n=1)
            P.op("act", lambda e: e.activation(out=st[:, c_rs:c_rs + 1], in_=st[:, c_ln:c_ln + 1], func=AF.Exp, scale=-0.5),
                 reads=[("st", c_ln)], writes=[("st", c_rs)], n=1)
            return c_rs

        def norm_transpose_phase(xt, xk, gb, gkey, sc, tbanks, hT, hk):
            for b in range(NB):
                hb = hbuf[b % 2]
                c_rs = rstd_ops(xt, xk, b, hb[:], ("hbuf", b % 2), sc)
                P.op("dve", lambda e, b=b, hb=hb, c_rs=c_rs: e.scalar_tensor_tensor(
                    out=hb[:], in0=xt[:, b, :], scalar=st[:, c_rs:c_rs + 1], in1=gb[:], op0=ALU.mult, op1=ALU.mult),
                    reads=[(xk, b), ("st", c_rs), gkey], writes=[("hbuf", b % 2)], n=D)
                bk = tbanks[b % 2]
                pst = banks[bk][:].bitcast(BF16)

                def tr(e, hb=hb, pst=pst):
                    ins = None
                    for kc in range(8):
                        ins = e.transpose(out=pst[:, kc * 128:(kc + 1) * 128], in_=hb[:, kc * 128:(kc + 1) * 128], identity=ident[:])
                    return ins
                P.op("pe", tr, reads=[("hbuf", b % 2), ("consts",)], writes=[("ps", bk)], n=1024, k=8)
                P.op("act", lambda e, b=b, pst=pst: e.activation(
                    out=hT[:, :, b * 128:(b + 1) * 128], in_=pst.rearrange("p (k t) -> p k t", k=8), func=AF.Copy),
                    reads=[("ps", bk)], writes=[(hk, b)], n=1024)

        def mm_group(out_ap, pairs, reads, bk):
            def f(e):
                ins = None
                n = len(pairs)
                for i, (l, r, _) in enumerate(pairs):
                    ins = e.matmul(out_ap, lhsT=l, rhs=r, start=(i == 0), stop=(i == n - 1))
                return ins
            P.op("pe", f, reads=reads, writes=[("ps", bk)], n=sum(p[2] for p in pairs), k=len(pairs))

        NPT = NT * NPIECE

        def load_wu(gp):
            slot, pc = gp % NSLOT, gp % NPIECE
            P.op("sp", lambda e: e.dma_start(
                out=wu[slot][:], in_=wup_s.rearrange("(kc p) f -> p kc f", p=128)[:, :, pc * 512:(pc + 1) * 512]),
                reads=[("wup_s",)], writes=[("wu", slot)], lane=f"wu{slot}", nbytes=1 << 20)

        def load_wd(gp):
            slot, pc = gp % NSLOT, gp % NPIECE
            P.op("sp", lambda e: e.dma_start(
                out=wd[slot][:], in_=wdn_s[pc * 512:(pc + 1) * 512, :].rearrange("(fc p) d -> p fc d", p=128)),
                reads=[("wdn_s",)], writes=[("wd", slot)], lane=f"wd{slot}", nbytes=1 << 20)

        def mixer(ti):
            tpos = ti % TPS
            par = ti % 2
            r0 = ti * T
            t0 = tpos * T
            x = xs[par]
            xk = "x%d" % par
            hT1 = hTs[0]
            hk = "hTa"
            H1_ALL = [(hk, b) for b in range(NB)]
            P.cur_tag = (ti, 'L')
            for b in range(NB):
                P.op("sp", lambda e, b=b: e.dma_start(out=x[:, b, :], in_=xin[r0 + b * 128: r0 + (b + 1) * 128, :]),
                     writes=[(xk, b)], lane=f"xld{par}{b}", nbytes=1 << 19)
            P.op("sp", lambda e: e.dma_start(out=cosb[:], in_=cos_d[:, t0:t0 + T]), writes=[("cos",)], lane="cos", nbytes=1 << 18)
            P.op("sp", lambda e: e.dma_start(out=sinb[:], in_=sin_d[:, t0:t0 + T]), writes=[("sin",)], lane="sin", nbytes=1 << 18)
            if ti == 0:
                for gp in range(NSLOT):
                    load_wu(gp)
                    load_wd(gp)
            if DBG_STOP == 'prolog':
                return False
            P.cur_tag = (ti, 'A')
            norm_transpose_phase(x, xk, g1b, ("g1b",), 0, (MB[0], MB[1]), hT1, hk)
            if DBG_STOP == 'A':
                return False

            P.cur_tag = (ti, 'B')
            pbank = (MB[0], MB[1])
            rbank = (MB[2], MB[3])
            chunk_list = [("q", c) for c in range(4)] + [("k", kv) for kv in range(2)]
            for ci, (kind, j) in enumerate(chunk_list):
                col0 = (QC0 if kind == "q" else KC0) + j * 128
                bk = pbank[ci % 2]
                rb = rbank[ci % 2]
                pairs = [(wcat[:, kc, col0:col0 + 128], hT1[:, kc, :], T) for kc in range(8)]
                mm_group(banks[bk][:], pairs, [("wcat",)] + H1_ALL, bk)
                P.op("act", lambda e, bk=bk: e.activation(out=xb[:], in_=banks[bk][:], func=AF.Copy),
                     reads=[("ps", bk)], writes=[("dT", 0)], n=T)
                mm_group(banks[rb][:], [(rotm[:], xb[:], T)], [("dT", 0), ("consts",)], rb)
                P.op("dve", lambda e, bk=bk: e.tensor_tensor(out=ropeA[:, 0:T], in0=banks[bk][:], in1=cosb[:], op=ALU.mult),
                     reads=[("ps", bk), ("cos",)], writes=[("tmpA",)], n=T)
                P.op("dve", lambda e, rb=rb: e.tensor_tensor(out=ropeB[:, 0:T], in0=banks[rb][:], in1=sinb[:], op=ALU.mult),
                     reads=[("ps", rb), ("sin",)], writes=[("tmpB",)], n=T)
                if kind == "q":
                    dst, wkey = qT[:, j, :], [("qT", j)]
                else:
                    dst, wkey = kT[par][:, j, :], [("kT", par, j)]
                P.op("pool", lambda e, dst=dst: e.tensor_tensor(out=dst, in0=ropeA[:, 0:T], in1=ropeB[:, 0:T], op=ALU.add),
                     reads=[("tmpA",), ("tmpB",)], writes=wkey, n=T)
            vb = MB[0]

            def vmm(e):
                ins = None
                for b in range(NB):
                    for kc in range(8):
                        ins = e.matmul(banks[vb][:, b * 128:(b + 1) * 128], lhsT=hT1[:, kc, b * 128:(b + 1) * 128],
                                       rhs=wcat[:, kc, VC0:VC0 + 128], start=(kc == 0), stop=(kc == 7))
                return ins
            P.op("pe", vmm, reads=[("wcat",)] + H1_ALL, writes=[("ps", vb)], n=32 * 128, k=32)
            P.op("act", lambda e: e.activation(
                out=vaug[par][:, :, :, 0:64], in_=banks[vb][:].rearrange("p (b k d) -> p b k d", b=NB, k=2), func=AF.Copy),
                reads=[("ps", vb)], writes=[("vaug", par, b) for b in range(NB)], n=T)
            if DBG_STOP == 'B':
                return False

            P.cur_tag = (ti, 'C')
            for g in range(4):
                w = POOL_W[g]
                L = LU
                uu = ub[g % 2]
                ukey = ("ub", g % 2)
                bk = MB[g % 2]
                yb = MB[2 + (g % 2)]
                col0 = UC0 + g * 128
                pairs = [(wcat[:, kc, col0:col0 + 128], hT1[:, kc, :], T) for kc in range(8)]
                mm_group(banks[bk][:], pairs, [("wcat",)] + H1_ALL, bk)
                if tpos == 0:
                    P.op("pool", lambda e, uu=uu: e.memset(uu[:, 0:16], 0.0), writes=[ukey], n=16)
                else:
                    P.op("pool", lambda e, uu=uu, g=g: e.tensor_copy(out=uu[:, 0:16], in_=hsave[:, g, :]),
                         reads=[("hsave", g)], writes=[ukey], n=16)
                P.op("act", lambda e, bk=bk, uu=uu, w=w: e.activation(out=uu[:, 16:L], in_=banks[bk][:], func=AF.Copy, scale=1.0 / w),
                     reads=[("ps", bk)], writes=[ukey], n=T)
                P.op("pool", lambda e, uu=uu, g=g: e.tensor_copy(out=hsave[:, g, :], in_=uu[:, T:T + 16]),
                     reads=[ukey], writes=[("hsave", g)], n=16)
                P.op("pool", lambda e, uu=uu: e.tensor_tensor(out=sA[:, 1:L], in0=uu[:, 1:L], in1=uu[:, 0:L - 1], op=ALU.add),
                     reads=[ukey], writes=[("tmpA",)], n=L)
                S_t, S_key = sA, ("tmpA",)
                if w >= 4:
                    P.op("pool", lambda e: e.tensor_tensor(out=sB[:, 3:L], in0=sA[:, 3:L], in1=sA[:, 1:L - 2], op=ALU.add),
                         reads=[("tmpA",)], writes=[("tmpB",)], n=L)
                    S_t, S_key = sB, ("tmpB",)
                if w >= 8:
                    P.op("pool", lambda e: e.tensor_tensor(out=sA[:, 7:L], in0=sB[:, 7:L], in1=sB[:, 3:L - 4], op=ALU.add),
                         reads=[("tmpB",)], writes=[("tmpA",)], n=L)
                    S_t, S_key = sA, ("tmpA",)
                if w >= 16:
                    P.op("pool", lambda e: e.tensor_tensor(out=sB[:, 15:L], in0=sA[:, 15:L], in1=sA[:, 7:L - 8], op=ALU.add),
                         reads=[("tmpA",)], writes=[("tmpB",)], n=L)
                    S_t, S_key = sB, ("tmpB",)
                dd = dT[g % 2]
                P.op("dve", lambda e, S_t=S_t, dd=dd, bk=bk: e.tensor_tensor(
                    out=dd[:], in0=S_t[:, 16:L], in1=banks[bk][:], op=ALU.subtract),
                    reads=[S_key, ("ps", bk)], writes=[("dT", g % 2)], n=T)
                if tpos == 0:
                    P.op("pool", lambda e, g=g, S_t=S_t: e.tensor_tensor(
                        out=fix[:], in0=S_t[:, 16:32], in1=icnt[:, g * 16:(g + 1) * 16], op=ALU.mult),
                        reads=[S_key, ("icnt",)], writes=[("fix",)], n=16)
                    P.op("dve", lambda e, dd=dd, bk=bk: e.tensor_tensor(out=dd[:, 0:16], in0=fix[:], in1=banks[bk][:, 0:16], op=ALU.subtract),
                         reads=[("fix",), ("ps", bk)], writes=[("dT", g % 2)], n=16)
                mm_group(banks[yb][:], [(wpool[:, g, :], dd[:], T)], [("wpool",), ("dT", g % 2)], yb)
                P.op("act", lambda e, g=g, yb=yb: e.activation(out=yT[:, g, :], in_=banks[yb][:], func=AF.Copy, scale=psct[:, g:g + 1]),
                     reads=[("ps", yb), ("psct",)], writes=[("yT", g)], n=T)
            if DBG_STOP == 'C':
                return False

            P.cur_tag = (ti, 'D')
            dent, denr = tmpA, tmpB
            groups = ([-1] if tpos > 0 else []) + [0, 1, 2, 3]
            for kb in groups:
                gpar = kb % 2
                if kb == -1:
                    ksrc, kcol, kpar = kT[1 - par], 3 * 128, 1 - par
                    q0, nq, pcol = 0, 128, 128
                else:
                    ksrc, kcol, kpar = kT[par], kb * 128, par
                    q0 = kb * 128
                    nq = 256 if kb < 3 else 128
                    pcol = 0
                for kv in range(2):
                    ptile = PT[gpar][kv]
                    bX, bY = MB[0], MB[1]

                    def smm(e, ksrc=ksrc, kcol=kcol, kv=kv, q0=q0, nq=nq):
                        ins = None
                        for hp in range(2):
                            c = kv * 2 + hp
                            for half in range(2):
                                bk = bX if half == 0 else bY
                                ins = e.matmul(banks[bk][:, hp * 256: hp * 256 + nq],
                                               lhsT=ksrc[half * 64:(half + 1) * 64, kv, kcol:kcol + 128],
                                               rhs=qT[half * 64:(half + 1) * 64, c, q0:q0 + nq], start=True, stop=True)
                        return ins
                    P.op("pe", smm, reads=[("kT", kpar, kv), ("qT", kv * 2), ("qT", kv * 2 + 1)],
                         writes=[("ps", bX), ("ps", bY)], n=2 * nq, k=4)
                    for half, bk in ((0, bX), (1, bY)):
                        P.op("act", lambda e, bk=bk, ptile=ptile, half=half, nq=nq, pcol=pcol: e.activation(
                            out=ptile[:, half::2, pcol:pcol + nq],
                            in_=banks[bk][:].rearrange("p (h q) -> p h q", h=2)[:, :, 0:nq], func=AF.Exp, scale=0.125),
                            reads=[("ps", bk)], writes=[("PT", gpar, kv, half)], n=2 * nq)
                    P.op("pool", lambda e, ptile=ptile, nq=nq, pcol=pcol: e.tensor_tensor(
                        out=ptile[:, :, pcol:pcol + nq], in0=ptile[:, :, pcol:pcol + nq],
                        in1=maskt[:, pcol:pcol + nq].unsqueeze(1).broadcast_to([128, 4, nq]), op=ALU.mult),
                        reads=[("PT", gpar, kv, 0), ("PT", gpar, kv, 1), ("consts",)],
                        writes=[("PT", gpar, kv, 0), ("PT", gpar, kv, 1)], n=4 * nq)
                if kb < 0:
                    continue
                b = kb
                has_prev = not (tpos == 0 and kb == 0)
                for kv in range(2):
                    ob = MB[2 + kv]
                    pairs = []
                    rd = [("PT", gpar, kv, 0), ("PT", gpar, kv, 1), ("vaug", par, b)]
                    if has_prev:
                        if kb == 0:
                            pv, pblk, ppar = vaug[1 - par], 3, 1 - par
                        else:
                            pv, pblk, ppar = vaug[par], kb - 1, par
                        pairs.append((pv[:, pblk, kv, :], PT[1 - gpar][kv][:, :, 128:256], 512))
                        rd += [("PT", 1 - gpar, kv, 0), ("PT", 1 - gpar, kv, 1), ("vaug", ppar, pblk)]
                    pairs.append((vaug[par][:, b, kv, :], PT[gpar][kv][:, :, 0:128], 512))
                    O3 = banks[ob][:].rearrange("p (h q) -> p h q", h=4)
                    mm_group(O3, pairs, rd, ob)
                    P.op("dve", lambda e, O3=O3, kv=kv: e.tensor_tensor(
                        out=dent[64:128, 0:512].rearrange("p (h q) -> p h q", h=4), in0=O3[64:128, :, :],
                        in1=est[64:128, kv * 4:(kv + 1) * 4].unsqueeze(2).broadcast_to([64, 4, 128]), op=ALU.add),
                        reads=[("ps", ob), ("est",)], writes=[("tmpA",)], n=512)
                    P.op("act", lambda e: e.activation(out=dent[64:128, 0:512], in_=dent[64:128, 0:512], func=AF.Ln),
                         reads=[("tmpA",)], writes=[("tmpA",)], n=512)
                    P.op("act", lambda e: e.activation(out=denr[0:64, 0:512], in_=dent[64:128, 0:512], func=AF.Exp, scale=-1.0),
                         reads=[("tmpA",)], writes=[("tmpB",)], n=512)
                    R3 = denr[:, 0:512].rearrange("p (h q) -> p h q", h=4)
                    for odd in range(2):
                        P.op("dve", lambda e, O3=O3, odd=odd, kv=kv, b=b, R3=R3: e.tensor_tensor(
                            out=attnT[odd * 64:(odd + 1) * 64, kv * 2:kv * 2 + 2, b * 128:(b + 1) * 128],
                            in0=O3[0:64, odd::2, :], in1=R3[0:64, odd::2, :], op=ALU.mult),
                            reads=[("ps", ob), ("tmpB",)], writes=[("attnT", kv, b, odd)], n=256)
            if DBG_STOP == 'D':
                return False

            P.cur_tag = (ti, 'E')
            oi = 0
            for b in range(NB):
                for half in range(2):
                    bk = MB[oi % 4]
                    oi += 1
                    pairs = []
                    for c in range(4):
                        pairs.append((attnT[:, c, b * 128:(b + 1) * 128], wout[:, c, half * 512:(half + 1) * 512], 512))
                    for g in range(4):
                        pairs.append((yT[:, g, b * 128:(b + 1) * 128], wout[:, 4 + g, half * 512:(half + 1) * 512], 512))
                    rd = [("wout",)] + [("attnT", kv, b, odd) for kv in range(2) for odd in range(2)] + [("yT", g) for g in range(4)]
                    mm_group(banks[bk][:], pairs, rd, bk)
                    P.op("dve", lambda e, b=b, half=half, bk=bk: e.tensor_tensor(
                        out=x[:, b, half * 512:(half + 1) * 512], in0=banks[bk][:], in1=x[:, b, half * 512:(half + 1) * 512], op=ALU.add),
                        reads=[("ps", bk), (xk, b)], writes=[(xk, b)], n=512)
            if DBG_STOP == 'E':
                return False
            P.cur_tag = (ti, 'F2')
            norm_transpose_phase(x, xk, g2b, ("g2b",), 12, (MB[2], MB[3]), hTs[1], "hTb")
            if DBG_STOP == 'F':
                return False
            return True

        def mlp(ti):
            par = ti % 2
            r0 = ti * T
            x = xs[par]
            xk = "x%d" % par
            hT2 = hTs[1]
            H2_ALL = [("hTb", b) for b in range(NB)]
            P.cur_tag = (ti, 'G')
            ui = 0
            di = 0
            for ga in range(NPIECE // 2):
                abuf = aT[ga % 2]
                slots = []
                gp0 = ti * NPIECE + ga * 2
                for pp in range(2):
                    gp = gp0 + pp
                    slot = gp % NSLOT
                    slots.append(slot)
                    for f in range(4):
                        fi = pp * 4 + f
                        bk = UB[ui % 2]
                        ui += 1
                        pairs = [(wu[slot][:, kc, f * 128:(f + 1) * 128], hT2[:, kc, :], T) for kc in range(8)]
                        mm_group(banks[bk][:], pairs, [("wu", slot)] + H2_ALL, bk)
                        P.op("act", lambda e, bk=bk: e.activation(out=banks[bk][:], in_=banks[bk][:], func=AF.Relu),
                             reads=[("ps", bk)], writes=[("ps", bk)], n=T)
                        P.op("act", lambda e, bk=bk, abuf=abuf, fi=fi: e.activation(out=abuf[:, fi, :], in_=banks[bk][:], func=AF.Square),
                             reads=[("ps", bk)], writes=[("aT", ga % 2, fi)], n=T)
                    if gp + NSLOT < NPT:
                        load_wu(gp + NSLOT)
                for b in range(NB):
                    for half in range(2):
                        bk = DB[di % 2]
                        di += 1
                        pairs = []
                        for fi in range(8):
                            pairs.append((abuf[:, fi, b * 128:(b + 1) * 128], wd[slots[fi // 4]][:, fi % 4, half * 512:(half + 1) * 512], 512))
                        rd = [("wd", s_) for s_ in slots] + [("aT", ga % 2, fi) for fi in range(8)]
                        mm_group(banks[bk][:], pairs, rd, bk)
                        P.op("dve", lambda e, b=b, half=half, bk=bk: e.tensor_tensor(
                            out=x[:, b, half * 512:(half + 1) * 512], in0=banks[bk][:], in1=x[:, b, half * 512:(half + 1) * 512], op=ALU.add),
                            reads=[("ps", bk), (xk, b)], writes=[(xk, b)], n=512)
                for gq in (gp0 + NSLOT, gp0 + NSLOT + 1):
                    if gq < NPT:
                        load_wd(gq)
            if DBG_STOP == 'G':
                return False
            P.cur_tag = (ti, 'H')
            for b in range(NB):
                c_rs = rstd_ops(x, xk, b, junkH[:], ("junkH",), 24)
                P.op("act", lambda e, b=b, c_rs=c_rs: e.activation(
                    out=x[:, b, :], in_=x[:, b, :], func=AF.Copy, scale=st[:, c_rs:c_rs + 1]),
                    reads=[(xk, b), ("st", c_rs)], writes=[(xk, b)], n=D)
                P.op("pool", lambda e, b=b: e.tensor_tensor(out=x[:, b, :], in0=x[:, b, :], in1=gfb[:], op=ALU.mult),
                     reads=[(xk, b), ("gfb",)], writes=[(xk, b)], n=D)
                P.op("sp", lambda e, b=b: e.dma_start(out=yout[r0 + b * 128: r0 + (b + 1) * 128, :], in_=x[:, b, :]),
                     reads=[(xk, b)], lane=f"st{par}{b}", nbytes=1 << 19)
            return True

        for ti in range(NT):
            if not mixer(ti):
                break
            if not mlp(ti):
                break

        if DO_SCHEDULE:
            P.schedule()
        P.finalize()
        final_waits = []
        for l, c in P.lane_counts.items():
            if l.startswith("st") or DBG_STOP is not None:
                final_waits.append(("lane:" + l, 16 * c))

        sem_names = ["eng:" + e for e in Prog.ENGS] + ["lane:" + l for l in P.lane_counts]
        sems = {}
        for i, n in enumerate(sem_names):
            sems[n] = es_.enter_context(nc.semaphore(f"s{i}"))
        block = es_.enter_context(nc.Block())
        P.emit(nc, sems, block, final_waits)
    return nc


def _consts():
    half = 32
    inv_freq = (np.float32(10000.0) ** (-np.arange(half, dtype=np.float32) / np.float32(half))).astype(np.float32)
    pos = np.arange(SEQ, dtype=np.float32)
    ang = (pos[:, None] * inv_freq[None, :]).astype(np.float32)
    cosv = np.cos(ang.astype(np.float64)).astype(np.float32)
    sinv = np.sin(ang.astype(np.float64)).astype(np.float32)
    fidx = (np.arange(128) % 64) % 32
    cosT = np.ascontiguousarray(cosv[:, fidx].T)
    sinT = np.ascontiguousarray(sinv[:, fidx].T)
    rot = np.zeros((128, 128), np.float32)
    for m in range(128):
        base = (m // 64) * 64
        d = m % 64
        if d < 32:
            rot[base + d + 32, m] = -1.0
        else:
            rot[base + d - 32, m] = 1.0
    ident = np.eye(128, dtype=np.float32)
    j = np.arange(128)[:, None]
    q = np.arange(128)[None, :]
    mask = np.concatenate([(j <= q), (j > q)], axis=1).astype(np.float32)
    t = np.arange(16)
    icn = np.concatenate([w / np.minimum(t + 1, w) for w in POOL_W]).astype(np.float32)[None, :]
    return cosT, sinT, rot, ident, mask, icn


_NC_CACHE = {}


def kernel(x, attn_norm_g, w_in, attn_sinks, w_pool, pool_scale, w_out,
           mlp_norm_g, w_up, w_down, final_norm_g):
    x = np.asarray(x, dtype=np.float32)
    w_in0 = np.asarray(w_in, dtype=np.float32)[0]
    cols = np.concatenate([
        np.arange(0, 512),
        np.arange(512, 576), np.arange(512, 576),
        np.arange(576, 640), np.arange(576, 640),
        np.arange(768, 1280),
        np.arange(640, 768),
    ])
    w_in_p = np.ascontiguousarray(w_in0[:, cols])
    cosT, sinT, rot, ident, mask, icn = _consts()
    shared = {
        "w_in_p": w_in_p,
        "w_out": np.ascontiguousarray(np.asarray(w_out, np.float32)[0]),
        "w_up": np.ascontiguousarray(np.asarray(w_up, np.float32)[0]),
        "w_down": np.ascontiguousarray(np.asarray(w_down, np.float32)[0]),
        "w_pool": np.ascontiguousarray(np.asarray(w_pool, np.float32)[0].reshape(512, 128)),
        "g1": np.ascontiguousarray(np.asarray(attn_norm_g, np.float32).reshape(1, D)),
        "g2": np.ascontiguousarray(np.asarray(mlp_norm_g, np.float32).reshape(1, D)),
        "gf": np.ascontiguousarray(np.asarray(final_norm_g, np.float32).reshape(1, D)),
        "pscale": np.ascontiguousarray(np.asarray(pool_scale, np.float32)[0].reshape(4, 128).T),
        "sinks": np.ascontiguousarray(np.asarray(attn_sinks, np.float32).reshape(1, 8)),
        "cosT": cosT, "sinT": sinT, "rotm": rot, "ident": ident, "maskcn": mask, "invcnt": icn,
    }
    in_maps = []
    for c in range(NCORES):
        m = dict(shared)
        m["x"] = np.ascontiguousarray(x[c * SEQ_PER_CORE:(c + 1) * SEQ_PER_CORE].reshape(TOK, D))
        in_maps.append(m)
    if "nc" not in _NC_CACHE:
        _NC_CACHE["nc"] = build_nc()
    nc = _NC_CACHE["nc"]
    res = run_bass_kernel_spmd(nc, in_maps, core_ids=list(range(NCORES)))
    out = np.concatenate([np.asarray(r["y"]).reshape(SEQ_PER_CORE, SEQ, D) for r in res.results], axis=0)
    return out.astype(np.float32, copy=False)
```

```python
import numpy as np
import concourse.bass as bass
import concourse.mybir as mybir
from concourse.bass_utils import run_bass_kernel_spmd

F32 = mybir.dt.float32
BF16 = mybir.dt.bfloat16
ALU = mybir.AluOpType
AF = mybir.ActivationFunctionType

NCORES = 8
D = 1024
SEQ = 2048
T = 512
NB = 4
SEQ_PER_CORE = 4
TOK = SEQ_PER_CORE * SEQ
NT = TOK // T
TPS = SEQ // T
NCOL = 1408
DFF = 4096
EPS = 1e-6
POOL_W = (2, 4, 8, 16)
NPIECE = 8
NSLOT = 3
DBG_STOP = None
DBG_OPLIMIT = None
DBG_PRINT = False
DO_SCHEDULE = True
OPT_JUNKH = False
OPT_PRIO1 = ('G', 'H')
OPT_FBANKS = 'DB'
OPT_LAT = 500.0
OPT_HSQ_DVE = True

QC0 = 0
KC0 = 512
UC0 = 768
VC0 = 1280


class _Op:
    __slots__ = ("eng", "fn", "deps", "lane", "idx", "sig", "waits", "is_dma", "cost", "nbytes", "tag", "est", "prio", "why")

    def __init__(self, eng, fn, lane):
        self.eng = eng
        self.fn = fn
        self.lane = lane
        self.is_dma = lane is not None
        self.deps = set()
        self.sig = None
        self.waits = []


class Prog:
    ENGS = ("sp", "act", "pool", "dve", "pe")

    def __init__(self):
        self.ops = []
        self.last_writer = {}
        self.readers = {}
        self.lane_counts = {}
        self.lane_waitall = set()
        self.cur_tag = None

    PER = {"act": (130.0, 1.25), "dve": (200.0, 1.3), "pool": (150.0, 2.5), "pe": (6.0, 0.42), "sp": (100.0, 0.0)}

    def op(self, eng, fn, reads=(), writes=(), lane=None, waitall=False, n=0, k=1, nbytes=0):
        if DBG_OPLIMIT is not None and len(self.ops) >= DBG_OPLIMIT:
            return None
        if DBG_PRINT:
            print("OP", len(self.ops), eng, "R", list(reads)[:3], "W", list(writes)[:3], flush=True)
        writes = list(writes) + [r for r in reads if r[0] == "ps" and r not in writes]
        reads = [r for r in reads if r[0] != "ps"]
        o = _Op(eng, fn, lane)
        o.idx = len(self.ops)
        fx, pe_ = self.PER[eng]
        o.cost = k * fx + n * pe_
        o.nbytes = nbytes
        o.tag = self.cur_tag
        o.prio = 1 if (self.cur_tag is not None and self.cur_tag[1] in OPT_PRIO1) else 0
        o.est = 0.0
        for r in reads:
            w = self.last_writer.get(r)
            if w is not None:
                o.deps.add(w)
        for r in writes:
            w = self.last_writer.get(r)
            if w is not None:
                o.deps.add(w)
            for rd in self.readers.get(r, ()):
                o.deps.add(rd)
        for r in writes:
            self.last_writer[r] = o.idx
            self.readers[r] = []
        for r in reads:
            if r not in writes:
                self.readers.setdefault(r, []).append(o.idx)
        o.deps.discard(o.idx)
        if lane is not None and waitall:
            o.deps = {d for d in o.deps if self.ops[d].lane != lane}
        if lane is not None:
            self.lane_counts[lane] = self.lane_counts.get(lane, 0) + 1
            o.sig = ("lane:" + lane, 16 * self.lane_counts[lane])
            if waitall:
                self.lane_waitall.add(lane)
        self.ops.append(o)
        return o

    def schedule(self):
        import heapq
        ops = self.ops
        n = len(ops)
        succ = [[] for _ in range(n)]
        indeg = [0] * n
        for o in ops:
            for d in o.deps:
                succ[d].append(o.idx)
                indeg[o.idx] += 1
        ready = [0.0] * n
        rdep = [None] * n
        lastop = {e: None for e in self.ENGS}
        pending = {e: [] for e in self.ENGS}
        avail = {e: [] for e in self.ENGS}
        for o in ops:
            if indeg[o.idx] == 0:
                heapq.heappush(pending[o.eng], (0.0, o.idx))
        free = {e: 0.0 for e in self.ENGS}
        dma_free = 0.0
        order = []
        while len(order) < n:
            best = None
            for e in self.ENGS:
                pe_, av = pending[e], avail[e]
                while pe_ and pe_[0][0] <= free[e]:
                    i_ = heapq.heappop(pe_)[1]
                    heapq.heappush(av, (ops[i_].prio, i_))
                if av:
                    cand = (free[e], av[0][1], e, True)
                elif pe_:
                    cand = (pe_[0][0], pe_[0][1], e, False)
                else:
                    continue
                if best is None or cand[:2] < best[:2]:
                    best = cand
            start, idx, e, from_av = best
            if from_av:
                heapq.heappop(avail[e])
            else:
                heapq.heappop(pending[e])
            o = ops[idx]
            if o.is_dma:
                free[e] = start + (1000.0 if e == "pool" else 100.0)
                d0 = max(start, dma_free)
                dma_free = d0 + o.nbytes / 140.0
                fin = dma_free + 1800.0
            else:
                free[e] = start + o.cost
                fin = free[e]
            order.append(idx)
            o.est = start
            o.why = ('dep', rdep[idx]) if (ready[idx] >= start - 1e-6 and rdep[idx] is not None) else ('eng', lastop[e])
            lastop[e] = idx
            for sidx in succ[idx]:
                so = ops[sidx]
                lat = 80.0 if (so.eng == e and not o.is_dma) else OPT_LAT
                if fin + lat > ready[sidx]:
                    ready[sidx] = fin + lat
                    rdep[sidx] = idx
                indeg[sidx] -= 1
                if indeg[sidx] == 0:
                    heapq.heappush(pending[so.eng], (ready[sidx], sidx))
        self.est_total = max(free.values())
        self.sched_old_ops = ops
        newidx = {old: new for new, old in enumerate(order)}
        new_ops = [ops[i] for i in order]
        for o in new_ops:
            o.deps = {newidx[d] for d in o.deps}
            o.idx = newidx[o.idx]
        self.ops = new_ops

    def finalize(self):
        ops = self.ops
        needed = set()
        for o in ops:
            for d in o.deps:
                do = ops[d]
                if do.is_dma:
                    continue
                if do.eng == "pe" and o.eng == "pe" and not o.is_dma:
                    continue
                needed.add(d)
        cnt = {e: 0 for e in self.ENGS}
        for o in ops:
            if o.is_dma:
                continue
            if o.idx in needed:
                cnt[o.eng] += 1
                o.sig = ("eng:" + o.eng, cnt[o.eng])
        seen = {e: {} for e in self.ENGS}
        for o in ops:
            req = {}
            for d in o.deps:
                do = ops[d]
                if (not do.is_dma) and do.eng == "pe" and o.eng == "pe" and not o.is_dma:
                    continue
                key, val = do.sig
                if do.is_dma and do.lane in self.lane_waitall:
                    val = 16 * self.lane_counts[do.lane]
                if val > req.get(key, 0):
                    req[key] = val
            s = seen[o.eng]
            for key, val in req.items():
                if s.get(key, 0) >= val:
                    continue
                s[key] = val
                o.waits.append((key, val))

    def emit(self, nc, sems, block, final_waits):
        per = {e: [o for o in self.ops if o.eng == e] for e in self.ENGS}

        def run(e, name):
            for o in per[name]:
                for key, val in o.waits:
                    e.wait_ge(sems[key], val)
                ins = o.fn(e)
                if o.sig is not None:
                    ins.then_inc(sems[o.sig[0]], 16 if o.is_dma else 1)
            if name == "sp":
                for key, val in final_waits:
                    e.wait_ge(sems[key], val)

        @block.sync
        def _(e):
            run(e, "sp")

        @block.scalar
        def _(e):
            run(e, "act")

        @block.gpsimd
        def _(e):
            run(e, "pool")

        @block.vector
        def _(e):
            run(e, "dve")

        @block.tensor
        def _(e):
            run(e, "pe")


def build_nc():
    nc = bass.Bass("TRN2", target_bir_lowering=False)
    P = Prog()

    def din(name, shape, dt=F32):
        return nc.dram_tensor(name, list(shape), dt, kind="ExternalInput").ap()

    xin = din("x", [TOK, D])
    w_in_p = din("w_in_p", [D, NCOL])
    w_out_d = din("w_out", [D, D])
    w_up_d = din("w_up", [D, DFF])
    w_dn_d = din("w_down", [DFF, D])
    w_pool_d = din("w_pool", [512, 128])
    g1_d = din("g1", [1, D])
    g2_d = din("g2", [1, D])
    gf_d = din("gf", [1, D])
    psc_d = din("pscale", [128, 4])
    sink_d = din("sinks", [1, 8])
    cos_d = din("cosT", [128, SEQ])
    sin_d = din("sinT", [128, SEQ])
    rot_d = din("rotm", [128, 128])
    idn_d = din("ident", [128, 128])
    msk_d = din("maskcn", [128, 256])
    icn_d = din("invcnt", [1, 64])
    yout = nc.dram_tensor("y", [TOK, D], F32, kind="ExternalOutput").ap()
    wup_s = nc.dram_tensor("wup_s", [D, DFF], BF16, kind="Internal").ap()
    wdn_s = nc.dram_tensor("wdn_s", [DFF, D], BF16, kind="Internal").ap()

    import contextlib
    es_ = contextlib.ExitStack()
    with es_:
        def sb(name, shape, dt):
            return es_.enter_context(nc.sbuf_tensor(name, list(shape), dt))

        LU = 16 + T
        wcat = sb("wcat", [128, 8, NCOL], BF16)
        wout = sb("wout", [128, 8, D], BF16)
        wpool = sb("wpool", [128, 4, 128], BF16)
        wu = [sb(f"wu{i}", [128, 8, 512], BF16) for i in range(NSLOT)]
        wd = [sb(f"wd{i}", [128, 4, D], BF16) for i in range(NSLOT)]
        xs = [sb(f"xres{i}", [128, NB, D], F32) for i in range(2)]
        hT1 = sb("hT1", [128, 8, T], BF16)
        hT2 = sb("hT2", [128, 8, T], BF16)
        junkH = sb("junkH", [128, D], BF16) if OPT_JUNKH else None
        hbuf = [sb(f"hbuf{i}", [128, D], BF16) for i in range(2)]
        qT = sb("qT", [128, 4, T], BF16)
        kT = [sb(f"kT{i}", [128, 2, T], BF16) for i in range(2)]
        vaug = [sb(f"vaug{i}", [128, NB, 2, 128], BF16) for i in range(2)]
        ub = [sb(f"ub{i}", [128, LU], F32) for i in range(2)]
        hsave = sb("hsave", [128, 4, 16], F32)
        tmpA = sb("tmpA", [128, LU], F32)
        tmpB = sb("tmpB", [128, LU], F32)
        dT = [sb(f"dT{i}", [128, T], BF16) for i in range(2)]
        yT = sb("yT", [128, 4, T], BF16)
        attnT = sb("attnT", [128, 4, T], BF16)
        PT = [[sb(f"PT{i}{k}", [128, 4, 256], BF16) for k in range(2)] for i in range(2)]
        aT = [sb(f"aT{i}", [128, 8, T], BF16) for i in range(2)]
        cosb = sb("cosb", [128, T], F32)
        sinb = sb("sinb", [128, T], F32)
        g1b = sb("g1b", [128, D], BF16)
        g2b = sb("g2b", [128, D], BF16)
        gfb = sb("gfb", [128, D], BF16)
        maskt = sb("maskt", [128, 256], BF16)
        rotm = sb("rotm_s", [128, 128], BF16)
        ident = sb("ident_s", [128, 128], BF16)
        sinkt = sb("sinkt", [128, 8], F32)
        est = sb("est", [128, 8], F32)
        psct = sb("psct", [128, 4], F32)
        icnt = sb("icnt", [128, 64], F32)
        st = sb("stats", [128, 48], F32)
        fix = sb("fixt", [128, 16], F32)
        xb = dT[0]
        ropeA, ropeB = tmpA, tmpB
        sA, sB = tmpA, tmpB

        banks = [es_.enter_context(nc.psum_tensor(f"bank{i}", [128, 512], F32)) for i in range(8)]
        UB = (0, 1)
        DB = (2, 3)
        MB = (4, 5, 6, 7)

        PL = "prolog"
        PW = "prolog_w"
        for kc in range(8):
            P.op("pool", lambda e, kc=kc: e.dma_start(out=wcat[:, kc, :], in_=w_in_p[kc * 128:(kc + 1) * 128, :]),
                 writes=[("wcat",)], lane=PL, waitall=True, nbytes=128 * NCOL * 4)
        for kc in range(8):
            P.op("pool", lambda e, kc=kc: e.dma_start(out=wout[:, kc, :], in_=w_out_d[kc * 128:(kc + 1) * 128, :]),
                 writes=[("wout",)], lane=PL, waitall=True, nbytes=128 * D * 4)
        for g in range(4):
            P.op("pool", lambda e, g=g: e.dma_start(out=wpool[:, g, :], in_=w_pool_d[g * 128:(g + 1) * 128, :]),
                 writes=[("wpool",)], lane=PL, waitall=True, nbytes=65536)
        P.op("pool", lambda e: e.dma_start(out=rotm[:], in_=rot_d[:, :]), writes=[("consts",)], lane=PL, waitall=True, nbytes=65536)
        P.op("pool", lambda e: e.dma_start(out=ident[:], in_=idn_d[:, :]), writes=[("consts",)], lane=PL, waitall=True, nbytes=65536)
        P.op("pool", lambda e: e.dma_start(out=maskt[:], in_=msk_d[:, :]), writes=[("consts",)], lane=PL, waitall=True, nbytes=131072)
        P.op("pool", lambda e: e.dma_start(out=g1b[:], in_=g1_d.partition_broadcast(128)), writes=[("g1b",)], lane=PL, waitall=True, nbytes=4096)
        P.op("pool", lambda e: e.dma_start(out=g2b[:], in_=g2_d.partition_broadcast(128)), writes=[("g2b",)], lane=PL, waitall=True, nbytes=4096)
        P.op("pool", lambda e: e.dma_start(out=gfb[:], in_=gf_d.partition_broadcast(128)), writes=[("gfb",)], lane=PL, waitall=True, nbytes=4096)
        for i in range(8):
            P.op("pool", lambda e, i=i: e.dma_start(out=wup_s[i * 128:(i + 1) * 128, :], in_=w_up_d[i * 128:(i + 1) * 128, :]),
                 writes=[("wup_s",)], lane=PW, waitall=True, nbytes=128 * DFF * 4)
        for i in range(8):
            P.op("pool", lambda e, i=i: e.dma_start(out=wdn_s[i * 512:(i + 1) * 512, :], in_=w_dn_d[i * 512:(i + 1) * 512, :]),
                 writes=[("wdn_s",)], lane=PW, waitall=True, nbytes=512 * D * 4)
        PS = "prolog_sp"
        P.op("sp", lambda e: e.dma_start(out=sinkt[:], in_=sink_d.partition_broadcast(128)), writes=[("sinkt",)], lane=PS, waitall=True, nbytes=4096)
        P.op("sp", lambda e: e.dma_start(out=psct[:], in_=psc_d[:, :]), writes=[("psct",)], lane=PS, waitall=True, nbytes=2048)
        P.op("sp", lambda e: e.dma_start(out=icnt[:], in_=icn_d.partition_broadcast(128)), writes=[("icnt",)], lane=PS, waitall=True, nbytes=32768)
        P.op("act", lambda e: e.activation(out=est[:], in_=sinkt[:], func=AF.Exp), reads=[("sinkt",)], writes=[("est",)], n=8)
        for i in range(2):
            P.op("pool", lambda e, i=i: e.memset(vaug[i][:], 1.0), writes=[("vaug", i, b) for b in range(NB)], n=1024)

        def rstd_ops(xt, xk, b, junk, junk_key, sc):
            c_ss, c_ln, c_rs = sc + b, sc + 4 + b, sc + 8 + b
            P.op("act", lambda e: e.activation(out=junk, in_=xt[:, b, :], func=AF.Square, accum_out=st[:, c_ss:c_ss + 1]),
                 reads=[(xk, b)], writes=[junk_key, ("st", c_ss)], n=D)
            P.op("act", lambda e: e.activation(out=st[:, c_ln:c_ln + 1], in_=st[:, c_ss:c_ss + 1], func=AF.Ln,
                                               scale=1.0 / D, bias=EPS),
                 reads=[("st", c_ss)], writes=[("st", c_ln)], n=1)
            P.op("act", lambda e: e.activation(out=st[:, c_rs:c_rs + 1], in_=st[:, c_ln:c_ln + 1], func=AF.Exp, scale=-0.5),
                 reads=[("st", c_ln)], writes=[("st", c_rs)], n=1)
            return c_rs

        def norm_transpose_phase(xt, xk, gb, gkey, sc, tbanks, hT, hk):
            for b in range(NB):
                hb = hbuf[b % 2]
                c_rs = rstd_ops(xt, xk, b, hb[:], ("hbuf", b % 2), sc)
                P.op("dve", lambda e, b=b, hb=hb, c_rs=c_rs: e.scalar_tensor_tensor(
                    out=hb[:], in0=xt[:, b, :], scalar=st[:, c_rs:c_rs + 1], in1=gb[:], op0=ALU.mult, op1=ALU.mult),
                    reads=[(xk, b), ("st", c_rs), gkey], writes=[("hbuf", b % 2)], n=D)
                bk = tbanks[b % 2]
                pst = banks[bk][:].bitcast(BF16)

                def tr(e, hb=hb, pst=pst):
                    ins = None
                    for kc in range(8):
                        ins = e.transpose(out=pst[:, kc * 128:(kc + 1) * 128], in_=hb[:, kc * 128:(kc + 1) * 128], identity=ident[:])
                    return ins
                P.op("pe", tr, reads=[("hbuf", b % 2), ("consts",)], writes=[("ps", bk)], n=1024, k=8)
                P.op("act", lambda e, b=b, pst=pst: e.activation(
                    out=hT[:, :, b * 128:(b + 1) * 128], in_=pst.rearrange("p (k t) -> p k t", k=8), func=AF.Copy),
                    reads=[("ps", bk)], writes=[(hk, b)], n=1024)

        def mm_group(out_ap, pairs, reads, bk):
            def f(e):
                ins = None
                n = len(pairs)
                for i, (l, r, _) in enumerate(pairs):
                    ins = e.matmul(out_ap, lhsT=l, rhs=r, start=(i == 0), stop=(i == n - 1))
                return ins
            P.op("pe", f, reads=reads, writes=[("ps", bk)], n=sum(p[2] for p in pairs), k=len(pairs))

        H1_ALL = [("hT1", b) for b in range(NB)]
        H2_ALL = [("hT2", b) for b in range(NB)]
        NPT = NT * NPIECE

        def load_wu(gp):
            slot, pc = gp % NSLOT, gp % NPIECE
            P.op("sp", lambda e: e.dma_start(
                out=wu[slot][:], in_=wup_s.rearrange("(kc p) f -> p kc f", p=128)[:, :, pc * 512:(pc + 1) * 512]),
                reads=[("wup_s",)], writes=[("wu", slot)], lane=f"wu{slot}", nbytes=1 << 20)

        def load_wd(gp):
            slot, pc = gp % NSLOT, gp % NPIECE
            P.op("sp", lambda e: e.dma_start(
                out=wd[slot][:], in_=wdn_s[pc * 512:(pc + 1) * 512, :].rearrange("(fc p) d -> p fc d", p=128)),
                reads=[("wdn_s",)], writes=[("wd", slot)], lane=f"wd{slot}", nbytes=1 << 20)

        def mixer(ti):
            tpos = ti % TPS
            par = ti % 2
            r0 = ti * T
            t0 = tpos * T
            x = xs[par]
            xk = "x%d" % par
            P.cur_tag = (ti, 'L')
            for b in range(NB):
                P.op("sp", lambda e, b=b: e.dma_start(out=x[:, b, :], in_=xin[r0 + b * 128: r0 + (b + 1) * 128, :]),
                     writes=[(xk, b)], lane=f"xld{par}{b}", nbytes=1 << 19)
            P.op("sp", lambda e: e.dma_start(out=cosb[:], in_=cos_d[:, t0:t0 + T]), writes=[("cos",)], lane="cos", nbytes=1 << 18)
            P.op("sp", lambda e: e.dma_start(out=sinb[:], in_=sin_d[:, t0:t0 + T]), writes=[("sin",)], lane="sin", nbytes=1 << 18)
            if ti == 0:
                for gp in range(NSLOT):
                    load_wu(gp)
                    load_wd(gp)
            if DBG_STOP == 'prolog':
                return False
            P.cur_tag = (ti, 'A')
            norm_transpose_phase(x, xk, g1b, ("g1b",), 0, (MB[0], MB[1]), hT1, "hT1")
            if DBG_STOP == 'A':
                return False

            P.cur_tag = (ti, 'B')
            pbank = (MB[0], MB[1])
            rbank = (MB[2], MB[3])
            chunk_list = [("q", c) for c in range(4)] + [("k", kv) for kv in range(2)]
            for ci, (kind, j) in enumerate(chunk_list):
                col0 = (QC0 if kind == "q" else KC0) + j * 128
                bk = pbank[ci % 2]
                rb = rbank[ci % 2]
                pairs = [(wcat[:, kc, col0:col0 + 128], hT1[:, kc, :], T) for kc in range(8)]
                mm_group(banks[bk][:], pairs, [("wcat",)] + H1_ALL, bk)
                P.op("act", lambda e, bk=bk: e.activation(out=xb[:], in_=banks[bk][:], func=AF.Copy),
                     reads=[("ps", bk)], writes=[("dT", 0)], n=T)
                mm_group(banks[rb][:], [(rotm[:], xb[:], T)], [("dT", 0), ("consts",)], rb)
                P.op("dve", lambda e, bk=bk: e.tensor_tensor(out=ropeA[:, 0:T], in0=banks[bk][:], in1=cosb[:], op=ALU.mult),
                     reads=[("ps", bk), ("cos",)], writes=[("tmpA",)], n=T)
                P.op("dve", lambda e, rb=rb: e.tensor_tensor(out=ropeB[:, 0:T], in0=banks[rb][:], in1=sinb[:], op=ALU.mult),
                     reads=[("ps", rb), ("sin",)], writes=[("tmpB",)], n=T)
                if kind == "q":
                    dst, wkey = qT[:, j, :], [("qT", j)]
                else:
                    dst, wkey = kT[par][:, j, :], [("kT", par, j)]
                P.op("pool", lambda e, dst=dst: e.tensor_tensor(out=dst, in0=ropeA[:, 0:T], in1=ropeB[:, 0:T], op=ALU.add),
                     reads=[("tmpA",), ("tmpB",)], writes=wkey, n=T)
            vb = MB[0]

            def vmm(e):
                ins = None
                for b in range(NB):
                    for kc in range(8):
                        ins = e.matmul(banks[vb][:, b * 128:(b + 1) * 128], lhsT=hT1[:, kc, b * 128:(b + 1) * 128],
                                       rhs=wcat[:, kc, VC0:VC0 + 128], start=(kc == 0), stop=(kc == 7))
                return ins
            P.op("pe", vmm, reads=[("wcat",)] + H1_ALL, writes=[("ps", vb)], n=32 * 128, k=32)
            P.op("act", lambda e: e.activation(
                out=vaug[par][:, :, :, 0:64], in_=banks[vb][:].rearrange("p (b k d) -> p b k d", b=NB, k=2), func=AF.Copy),
                reads=[("ps", vb)], writes=[("vaug", par, b) for b in range(NB)], n=T)
            if DBG_STOP == 'B':
                return False

            P.cur_tag = (ti, 'C')
            for g in range(4):
                w = POOL_W[g]
                L = LU
                uu = ub[g % 2]
                ukey = ("ub", g % 2)
                bk = MB[g % 2]
                yb = MB[2 + (g % 2)]
                col0 = UC0 + g * 128
                pairs = [(wcat[:, kc, col0:col0 + 128], hT1[:, kc, :], T) for kc in range(8)]
                mm_group(banks[bk][:], pairs, [("wcat",)] + H1_ALL, bk)
                if tpos == 0:
                    P.op("pool", lambda e, uu=uu: e.memset(uu[:, 0:16], 0.0), writes=[ukey], n=16)
                else:
                    P.op("pool", lambda e, uu=uu, g=g: e.tensor_copy(out=uu[:, 0:16], in_=hsave[:, g, :]),
                         reads=[("hsave", g)], writes=[ukey], n=16)
                P.op("act", lambda e, bk=bk, uu=uu, w=w: e.activation(out=uu[:, 16:L], in_=banks[bk][:], func=AF.Copy, scale=1.0 / w),
                     reads=[("ps", bk)], writes=[ukey], n=T)
                P.op("pool", lambda e, uu=uu, g=g: e.tensor_copy(out=hsave[:, g, :], in_=uu[:, T:T + 16]),
                     reads=[ukey], writes=[("hsave", g)], n=16)
                P.op("pool", lambda e, uu=uu: e.tensor_tensor(out=sA[:, 1:L], in0=uu[:, 1:L], in1=uu[:, 0:L - 1], op=ALU.add),
                     reads=[ukey], writes=[("tmpA",)], n=L)
                S_t, S_key = sA, ("tmpA",)
                if w >= 4:
                    P.op("pool", lambda e: e.tensor_tensor(out=sB[:, 3:L], in0=sA[:, 3:L], in1=sA[:, 1:L - 2], op=ALU.add),
                         reads=[("tmpA",)], writes=[("tmpB",)], n=L)
                    S_t, S_key = sB, ("tmpB",)
                if w >= 8:
                    P.op("pool", lambda e: e.tensor_tensor(out=sA[:, 7:L], in0=sB[:, 7:L], in1=sB[:, 3:L - 4], op=ALU.add),
                         reads=[("tmpB",)], writes=[("tmpA",)], n=L)
                    S_t, S_key = sA, ("tmpA",)
                if w >= 16:
                    P.op("pool", lambda e: e.tensor_tensor(out=sB[:, 15:L], in0=sA[:, 15:L], in1=sA[:, 7:L - 8], op=ALU.add),
                         reads=[("tmpA",)], writes=[("tmpB",)], n=L)
                    S_t, S_key = sB, ("tmpB",)
                dd = dT[g % 2]
                P.op("dve", lambda e, S_t=S_t, dd=dd, bk=bk: e.tensor_tensor(
                    out=dd[:], in0=S_t[:, 16:L], in1=banks[bk][:], op=ALU.subtract),
                    reads=[S_key, ("ps", bk)], writes=[("dT", g % 2)], n=T)
                if tpos == 0:
                    P.op("pool", lambda e, g=g, S_t=S_t: e.tensor_tensor(
                        out=fix[:], in0=S_t[:, 16:32], in1=icnt[:, g * 16:(g + 1) * 16], op=ALU.mult),
                        reads=[S_key, ("icnt",)], writes=[("fix",)], n=16)
                    P.op("dve", lambda e, dd=dd, bk=bk: e.tensor_tensor(out=dd[:, 0:16], in0=fix[:], in1=banks[bk][:, 0:16], op=ALU.subtract),
                         reads=[("fix",), ("ps", bk)], writes=[("dT", g % 2)], n=16)
                mm_group(banks[yb][:], [(wpool[:, g, :], dd[:], T)], [("wpool",), ("dT", g % 2)], yb)
                P.op("act", lambda e, g=g, yb=yb: e.activation(out=yT[:, g, :], in_=banks[yb][:], func=AF.Copy, scale=psct[:, g:g + 1]),
                     reads=[("ps", yb), ("psct",)], writes=[("yT", g)], n=T)
            if DBG_STOP == 'C':
                return False

            P.cur_tag = (ti, 'D')
            dent, denr = tmpA, tmpB
            groups = ([-1] if tpos > 0 else []) + [0, 1, 2, 3]
            for kb in groups:
                gpar = kb % 2
                if kb == -1:
                    ksrc, kcol, kpar = kT[1 - par], 3 * 128, 1 - par
                    q0, nq, pcol = 0, 128, 128
                else:
                    ksrc, kcol, kpar = kT[par], kb * 128, par
                    q0 = kb * 128
                    nq = 256 if kb < 3 else 128
                    pcol = 0
                for kv in range(2):
                    ptile = PT[gpar][kv]
                    bX, bY = MB[0], MB[1]

                    def smm(e, ksrc=ksrc, kcol=kcol, kv=kv, q0=q0, nq=nq):
                        ins = None
                        for hp in range(2):
                            c = kv * 2 + hp
                            for half in range(2):
                                bk = bX if half == 0 else bY
                                ins = e.matmul(banks[bk][:, hp * 256: hp * 256 + nq],
                                               lhsT=ksrc[half * 64:(half + 1) * 64, kv, kcol:kcol + 128],
                                               rhs=qT[half * 64:(half + 1) * 64, c, q0:q0 + nq], start=True, stop=True)
                        return ins
                    P.op("pe", smm, reads=[("kT", kpar, kv), ("qT", kv * 2), ("qT", kv * 2 + 1)],
                         writes=[("ps", bX), ("ps", bY)], n=2 * nq, k=4)
                    for half, bk in ((0, bX), (1, bY)):
                        P.op("act", lambda e, bk=bk, ptile=ptile, half=half, nq=nq, pcol=pcol: e.activation(
                            out=ptile[:, half::2, pcol:pcol + nq],
                            in_=banks[bk][:].rearrange("p (h q) -> p h q", h=2)[:, :, 0:nq], func=AF.Exp, scale=0.125),
                            reads=[("ps", bk)], writes=[("PT", gpar, kv, half)], n=2 * nq)
                    P.op("pool", lambda e, ptile=ptile, nq=nq, pcol=pcol: e.tensor_tensor(
                        out=ptile[:, :, pcol:pcol + nq], in0=ptile[:, :, pcol:pcol + nq],
                        in1=maskt[:, pcol:pcol + nq].unsqueeze(1).broadcast_to([128, 4, nq]), op=ALU.mult),
                        reads=[("PT", gpar, kv, 0), ("PT", gpar, kv, 1), ("consts",)],
                        writes=[("PT", gpar, kv, 0), ("PT", gpar, kv, 1)], n=4 * nq)
                if kb < 0:
                    continue
                b = kb
                has_prev = not (tpos == 0 and kb == 0)
                for kv in range(2):
                    ob = MB[2 + kv]
                    pairs = []
                    rd = [("PT", gpar, kv, 0), ("PT", gpar, kv, 1), ("vaug", par, b)]
                    if has_prev:
                        if kb == 0:
                            pv, pblk, ppar = vaug[1 - par], 3, 1 - par
                        else:
                            pv, pblk, ppar = vaug[par], kb - 1, par
                        pairs.append((pv[:, pblk, kv, :], PT[1 - gpar][kv][:, :, 128:256], 512))
                        rd += [("PT", 1 - gpar, kv, 0), ("PT", 1 - gpar, kv, 1), ("vaug", ppar, pblk)]
                    pairs.append((vaug[par][:, b, kv, :], PT[gpar][kv][:, :, 0:128], 512))
                    O3 = banks[ob][:].rearrange("p (h q) -> p h q", h=4)
                    mm_group(O3, pairs, rd, ob)
                    P.op("dve", lambda e, O3=O3, kv=kv: e.tensor_tensor(
                        out=dent[64:128, 0:512].rearrange("p (h q) -> p h q", h=4), in0=O3[64:128, :, :],
                        in1=est[64:128, kv * 4:(kv + 1) * 4].unsqueeze(2).broadcast_to([64, 4, 128]), op=ALU.add),
                        reads=[("ps", ob), ("est",)], writes=[("tmpA",)], n=512)
                    P.op("act", lambda e: e.activation(out=dent[64:128, 0:512], in_=dent[64:128, 0:512], func=AF.Ln),
                         reads=[("tmpA",)], writes=[("tmpA",)], n=512)
                    P.op("act", lambda e: e.activation(out=denr[0:64, 0:512], in_=dent[64:128, 0:512], func=AF.Exp, scale=-1.0),
                         reads=[("tmpA",)], writes=[("tmpB",)], n=512)
                    R3 = denr[:, 0:512].rearrange("p (h q) -> p h q", h=4)
                    for odd in range(2):
                        P.op("dve", lambda e, O3=O3, odd=odd, kv=kv, b=b, R3=R3: e.tensor_tensor(
                            out=attnT[odd * 64:(odd + 1) * 64, kv * 2:kv * 2 + 2, b * 128:(b + 1) * 128],
                            in0=O3[0:64, odd::2, :], in1=R3[0:64, odd::2, :], op=ALU.mult),
                            reads=[("ps", ob), ("tmpB",)], writes=[("attnT", kv, b, odd)], n=256)
            if DBG_STOP == 'D':
                return False

            P.cur_tag = (ti, 'E')
            oi = 0
            for b in range(NB):
                for half in range(2):
                    bk = MB[oi % 4]
                    oi += 1
                    pairs = []
                    for c in range(4):
                        pairs.append((attnT[:, c, b * 128:(b + 1) * 128], wout[:, c, half * 512:(half + 1) * 512], 512))
                    for g in range(4):
                        pairs.append((yT[:, g, b * 128:(b + 1) * 128], wout[:, 4 + g, half * 512:(half + 1) * 512], 512))
                    rd = [("wout",)] + [("attnT", kv, b, odd) for kv in range(2) for odd in range(2)] + [("yT", g) for g in range(4)]
                    mm_group(banks[bk][:], pairs, rd, bk)
                    P.op("dve", lambda e, b=b, half=half, bk=bk: e.tensor_tensor(
                        out=x[:, b, half * 512:(half + 1) * 512], in0=banks[bk][:], in1=x[:, b, half * 512:(half + 1) * 512], op=ALU.add),
                        reads=[("ps", bk), (xk, b)], writes=[(xk, b)], n=512)
            if DBG_STOP == 'E':
                return False
            return True

        def mlp(ti):
            par = ti % 2
            r0 = ti * T
            x = xs[par]
            xk = "x%d" % par
            P.cur_tag = (ti, 'F')
            norm_transpose_phase(x, xk, g2b, ("g2b",), 12, (DB[0], DB[1]) if OPT_FBANKS == 'DB' else (MB[2], MB[3]), hT2, "hT2")
            if DBG_STOP == 'F':
                return False
            P.cur_tag = (ti, 'G')
            ui = 0
            di = 0
            for ga in range(NPIECE // 2):
                abuf = aT[ga % 2]
                slots = []
                gp0 = ti * NPIECE + ga * 2
                for pp in range(2):
                    gp = gp0 + pp
                    slot = gp % NSLOT
                    slots.append(slot)
                    for f in range(4):
                        fi = pp * 4 + f
                        bk = UB[ui % 2]
                        ui += 1
                        pairs = [(wu[slot][:, kc, f * 128:(f + 1) * 128], hT2[:, kc, :], T) for kc in range(8)]
                        mm_group(banks[bk][:], pairs, [("wu", slot)] + H2_ALL, bk)
                        P.op("act", lambda e, bk=bk: e.activation(out=banks[bk][:], in_=banks[bk][:], func=AF.Relu),
                             reads=[("ps", bk)], writes=[("ps", bk)], n=T)
                        P.op("act", lambda e, bk=bk, abuf=abuf, fi=fi: e.activation(out=abuf[:, fi, :], in_=banks[bk][:], func=AF.Square),
                             reads=[("ps", bk)], writes=[("aT", ga % 2, fi)], n=T)
                    if gp + NSLOT < NPT:
                        load_wu(gp + NSLOT)
                for b in range(NB):
                    for half in range(2):
                        bk = DB[di % 2]
                        di += 1
                        pairs = []
                        for fi in range(8):
                            pairs.append((abuf[:, fi, b * 128:(b + 1) * 128], wd[slots[fi // 4]][:, fi % 4, half * 512:(half + 1) * 512], 512))
                        rd = [("wd", s_) for s_ in slots] + [("aT", ga % 2, fi) for fi in range(8)]
                        mm_group(banks[bk][:], pairs, rd, bk)
                        P.op("dve", lambda e, b=b, half=half, bk=bk: e.tensor_tensor(
                            out=x[:, b, half * 512:(half + 1) * 512], in0=banks[bk][:], in1=x[:, b, half * 512:(half + 1) * 512], op=ALU.add),
                            reads=[("ps", bk), (xk, b)], writes=[(xk, b)], n=512)
                for gq in (gp0 + NSLOT, gp0 + NSLOT + 1):
                    if gq < NPT:
                        load_wd(gq)
            if DBG_STOP == 'G':
                return False
            P.cur_tag = (ti, 'H')
            for b in range(NB):
                if OPT_JUNKH:
                    junk, jkey = junkH[:], ("junkH",)
                else:
                    junk, jkey = aT[0][:, 0:2, :], ("aT", 0, 0)
                if OPT_HSQ_DVE:
                    c_ss, c_ln, c_rs = 24 + b, 28 + b, 32 + b
                    P.op("dve", lambda e, b=b, junk=junk, c_ss=c_ss: e.scalar_tensor_tensor(
                        out=junk, in0=x[:, b, :], scalar=1.0, in1=x[:, b, :], op0=ALU.mult, op1=ALU.mult, accum_out=st[:, c_ss:c_ss + 1]),
                        reads=[(xk, b)], writes=[jkey, ("st", c_ss)], n=D)
                    P.op("act", lambda e, c_ss=c_ss, c_ln=c_ln: e.activation(out=st[:, c_ln:c_ln + 1], in_=st[:, c_ss:c_ss + 1], func=AF.Ln,
                                                                         scale=1.0 / D, bias=EPS),
                         reads=[("st", c_ss)], writes=[("st", c_ln)], n=1)
                    P.op("act", lambda e, c_ln=c_ln, c_rs=c_rs: e.activation(out=st[:, c_rs:c_rs + 1], in_=st[:, c_ln:c_ln + 1], func=AF.Exp, scale=-0.5),
                         reads=[("st", c_ln)], writes=[("st", c_rs)], n=1)
                else:
                    c_rs = rstd_ops(x, xk, b, junk, jkey, 24)
                P.op("act", lambda e, b=b, c_rs=c_rs: e.activation(
                    out=x[:, b, :], in_=x[:, b, :], func=AF.Copy, scale=st[:, c_rs:c_rs + 1]),
                    reads=[(xk, b), ("st", c_rs)], writes=[(xk, b)], n=D)
                P.op("pool", lambda e, b=b: e.tensor_tensor(out=x[:, b, :], in0=x[:, b, :], in1=gfb[:], op=ALU.mult),
                     reads=[(xk, b), ("gfb",)], writes=[(xk, b)], n=D)
                P.op("sp", lambda e, b=b: e.dma_start(out=yout[r0 + b * 128: r0 + (b + 1) * 128, :], in_=x[:, b, :]),
                     reads=[(xk, b)], lane=f"st{par}{b}", nbytes=1 << 19)
            return True

        for ti in range(NT):
            if not mixer(ti):
                break
            if not mlp(ti):
                break

        if DO_SCHEDULE:
            P.schedule()
        P.finalize()
        final_waits = []
        for l, c in P.lane_counts.items():
            if l.startswith("st") or DBG_STOP is not None:
                final_waits.append(("lane:" + l, 16 * c))

        sem_names = ["eng:" + e for e in Prog.ENGS] + ["lane:" + l for l in P.lane_counts]
        sems = {}
        for i, n in enumerate(sem_names):
            sems[n] = es_.enter_context(nc.semaphore(f"s{i}"))
        block = es_.enter_context(nc.Block())
        P.emit(nc, sems, block, final_waits)
    return nc


def _consts():
    half = 32
    inv_freq = 10000.0 ** (-np.arange(half, dtype=np.float64) / float(half))
    pos = np.arange(SEQ, dtype=np.float64)
    ang = pos[:, None] * inv_freq[None, :]
    cosv = np.cos(ang).astype(np.float32)
    sinv = np.sin(ang).astype(np.float32)
    fidx = (np.arange(128) % 64) % 32
    cosT = np.ascontiguousarray(cosv[:, fidx].T)
    sinT = np.ascontiguousarray(sinv[:, fidx].T)
    rot = np.zeros((128, 128), np.float32)
    for m in range(128):
        base = (m // 64) * 64
        d = m % 64
        if d < 32:
            rot[base + d + 32, m] = -1.0
        else:
            rot[base + d - 32, m] = 1.0
    ident = np.eye(128, dtype=np.float32)
    j = np.arange(128)[:, None]
    q = np.arange(128)[None, :]
    mask = np.concatenate([(j <= q), (j > q)], axis=1).astype(np.float32)
    t = np.arange(16)
    icn = np.concatenate([w / np.minimum(t + 1, w) for w in POOL_W]).astype(np.float32)[None, :]
    return cosT, sinT, rot, ident, mask, icn


_NC_CACHE = {}


def kernel(x, attn_norm_g, w_in, attn_sinks, w_pool, pool_scale, w_out,
           mlp_norm_g, w_up, w_down, final_norm_g):
    x = np.asarray(x, dtype=np.float32)
    w_in0 = np.asarray(w_in, dtype=np.float32)[0]
    cols = np.concatenate([
        np.arange(0, 512),
        np.arange(512, 576), np.arange(512, 576),
        np.arange(576, 640), np.arange(576, 640),
        np.arange(768, 1280),
        np.arange(640, 768),
    ])
    w_in_p = np.ascontiguousarray(w_in0[:, cols])
    cosT, sinT, rot, ident, mask, icn = _consts()
    shared = {
        "w_in_p": w_in_p,
        "w_out": np.ascontiguousarray(np.asarray(w_out, np.float32)[0]),
        "w_up": np.ascontiguousarray(np.asarray(w_up, np.float32)[0]),
        "w_down": np.ascontiguousarray(np.asarray(w_down, np.float32)[0]),
        "w_pool": np.ascontiguousarray(np.asarray(w_pool, np.float32)[0].reshape(512, 128)),
        "g1": np.ascontiguousarray(np.asarray(attn_norm_g, np.float32).reshape(1, D)),
        "g2": np.ascontiguousarray(np.asarray(mlp_norm_g, np.float32).reshape(1, D)),
        "gf": np.ascontiguousarray(np.asarray(final_norm_g, np.float32).reshape(1, D)),
        "pscale": np.ascontiguousarray(np.asarray(pool_scale, np.float32)[0].reshape(4, 128).T),
        "sinks": np.ascontiguousarray(np.asarray(attn_sinks, np.float32).reshape(1, 8)),
        "cosT": cosT, "sinT": sinT, "rotm": rot, "ident": ident, "maskcn": mask, "invcnt": icn,
    }
    in_maps = []
    for c in range(NCORES):
        m = dict(shared)
        m["x"] = np.ascontiguousarray(x[c * SEQ_PER_CORE:(c + 1) * SEQ_PER_CORE].reshape(TOK, D))
        in_maps.append(m)
    if "nc" not in _NC_CACHE:
        _NC_CACHE["nc"] = build_nc()
    nc = _NC_CACHE["nc"]
    res = run_bass_kernel_spmd(nc, in_maps, core_ids=list(range(NCORES)))
    out = np.concatenate([np.asarray(r["y"]).reshape(SEQ_PER_CORE, SEQ, D) for r in res.results], axis=0)
    return out.astype(np.float32, copy=False)
```

```python
import numpy as np
import concourse.bass as bass
import concourse.mybir as mybir
from concourse.bass_utils import run_bass_kernel_spmd

F32 = mybir.dt.float32
BF16 = mybir.dt.bfloat16
ALU = mybir.AluOpType
AF = mybir.ActivationFunctionType

NCORES = 8
D = 1024
SEQ = 2048
T = 512
NB = 4
SEQ_PER_CORE = 4
TOK = SEQ_PER_CORE * SEQ
NT = TOK // T
TPS = SEQ // T
NCOL = 1408
DFF = 4096
EPS = 1e-6
POOL_W = (2, 4, 8, 16)
NPIECE = 8
NSLOT = 3
DBG_STOP = None
DBG_OPLIMIT = None
DBG_PRINT = False
DO_SCHEDULE = True
OPT_JUNKH = False
OPT_PRIO1 = ('G', 'H')
OPT_FBANKS = 'DB'
OPT_LAT = 500.0
OPT_TILEPRIO = True
OPT_MIXLEAD = 1
OPT_XA = True
OPT_FY = True
OPT_FTOP = True
OPT_XAFROM = 2
OPT_LEADFROM = 2
OPT_MIXLEAD0 = 0.4
OPT_AT1 = True
OPT_HT2X2 = True
OPT_HSQ_DVE = True

QC0 = 0
KC0 = 512
UC0 = 768
VC0 = 1280


class _Op:
    __slots__ = ("eng", "fn", "deps", "lane", "idx", "sig", "waits", "is_dma", "cost", "nbytes", "tag", "est", "prio", "why")

    def __init__(self, eng, fn, lane):
        self.eng = eng
        self.fn = fn
        self.lane = lane
        self.is_dma = lane is not None
        self.deps = set()
        self.sig = None
        self.waits = []


class Prog:
    ENGS = ("sp", "act", "pool", "dve", "pe")

    def __init__(self):
        self.ops = []
        self.last_writer = {}
        self.readers = {}
        self.lane_counts = {}
        self.lane_waitall = set()
        self.cur_tag = None

    PER = {"act": (130.0, 1.25), "dve": (200.0, 1.3), "pool": (150.0, 2.5), "pe": (6.0, 0.42), "sp": (100.0, 0.0)}

    def op(self, eng, fn, reads=(), writes=(), lane=None, waitall=False, n=0, k=1, nbytes=0):
        if DBG_OPLIMIT is not None and len(self.ops) >= DBG_OPLIMIT:
            return None
        if DBG_PRINT:
            print("OP", len(self.ops), eng, "R", list(reads)[:3], "W", list(writes)[:3], flush=True)
        writes = list(writes) + [r for r in reads if r[0] == "ps" and r not in writes]
        reads = [r for r in reads if r[0] != "ps"]
        o = _Op(eng, fn, lane)
        o.idx = len(self.ops)
        fx, pe_ = self.PER[eng]
        o.cost = k * fx + n * pe_
        o.nbytes = nbytes
        o.tag = self.cur_tag
        if self.cur_tag is None:
            o.prio = -10
        elif OPT_TILEPRIO:
            if self.cur_tag[1] == 'F' and OPT_FTOP:
                o.prio = -5
            else:
                o.prio = (self.cur_tag[0] + 0.5) if self.cur_tag[1] in OPT_PRIO1 else (self.cur_tag[0] - (OPT_MIXLEAD if self.cur_tag[0] >= OPT_LEADFROM else OPT_MIXLEAD0))
        else:
            o.prio = 1 if self.cur_tag[1] in OPT_PRIO1 else 0
        o.est = 0.0
        for r in reads:
            w = self.last_writer.get(r)
            if w is not None:
                o.deps.add(w)
        for r in writes:
            w = self.last_writer.get(r)
            if w is not None:
                o.deps.add(w)
            for rd in self.readers.get(r, ()):
                o.deps.add(rd)
        for r in writes:
            self.last_writer[r] = o.idx
            self.readers[r] = []
        for r in reads:
            if r not in writes:
                self.readers.setdefault(r, []).append(o.idx)
        o.deps.discard(o.idx)
        if lane is not None and waitall:
            o.deps = {d for d in o.deps if self.ops[d].lane != lane}
        if lane is not None:
            self.lane_counts[lane] = self.lane_counts.get(lane, 0) + 1
            o.sig = ("lane:" + lane, 16 * self.lane_counts[lane])
            if waitall:
                self.lane_waitall.add(lane)
        self.ops.append(o)
        return o

    def schedule(self):
        import heapq
        ops = self.ops
        n = len(ops)
        succ = [[] for _ in range(n)]
        indeg = [0] * n
        for o in ops:
            for d in o.deps:
                succ[d].append(o.idx)
                indeg[o.idx] += 1
        ready = [0.0] * n
        rdep = [None] * n
        lastop = {e: None for e in self.ENGS}
        rdep = [None] * n
        lastop = {e: None for e in self.ENGS}
        pending = {e: [] for e in self.ENGS}
        avail = {e: [] for e in self.ENGS}
        for o in ops:
            if indeg[o.idx] == 0:
                heapq.heappush(pending[o.eng], (0.0, o.idx))
        free = {e: 0.0 for e in self.ENGS}
        dma_free = 0.0
        order = []
        while len(order) < n:
            best = None
            for e in self.ENGS:
                pe_, av = pending[e], avail[e]
                while pe_ and pe_[0][0] <= free[e]:
                    i_ = heapq.heappop(pe_)[1]
                    heapq.heappush(av, (ops[i_].prio, i_))
                if av:
                    cand = (free[e], av[0][1], e, True)
                elif pe_:
                    cand = (pe_[0][0], pe_[0][1], e, False)
                else:
                    continue
                if best is None or cand[:2] < best[:2]:
                    best = cand
            start, idx, e, from_av = best
            if from_av:
                heapq.heappop(avail[e])
            else:
                heapq.heappop(pending[e])
            o = ops[idx]
            if o.is_dma:
                free[e] = start + (1000.0 if e == "pool" else 100.0)
                d0 = max(start, dma_free)
                dma_free = d0 + o.nbytes / 140.0
                fin = dma_free + 1800.0
            else:
                free[e] = start + o.cost
                fin = free[e]
            order.append(idx)
            o.est = start
            o.why = ('dep', rdep[idx]) if (ready[idx] >= start - 1e-6 and rdep[idx] is not None) else ('eng', lastop[e])
            lastop[e] = idx
            o.why = ('dep', rdep[idx]) if (ready[idx] >= start - 1e-6 and rdep[idx] is not None) else ('eng', lastop[e])
            lastop[e] = idx
            for sidx in succ[idx]:
                so = ops[sidx]
                lat = 80.0 if (so.eng == e and not o.is_dma) else OPT_LAT
                if fin + lat > ready[sidx]:
                    ready[sidx] = fin + lat
                    rdep[sidx] = idx
                    rdep[sidx] = idx
                indeg[sidx] -= 1
                if indeg[sidx] == 0:
                    heapq.heappush(pending[so.eng], (ready[sidx], sidx))
        self.est_total = max(free.values())
        self.sched_old_ops = ops
        self.sched_old_ops = ops
        newidx = {old: new for new, old in enumerate(order)}
        new_ops = [ops[i] for i in order]
        for o in new_ops:
            o.deps = {newidx[d] for d in o.deps}
            o.idx = newidx[o.idx]
        self.ops = new_ops

    def finalize(self):
        ops = self.ops
        needed = set()
        for o in ops:
            for d in o.deps:
                do = ops[d]
                if do.is_dma:
                    continue
                if do.eng == "pe" and o.eng == "pe" and not o.is_dma:
                    continue
                needed.add(d)
        cnt = {e: 0 for e in self.ENGS}
        for o in ops:
            if o.is_dma:
                continue
            if o.idx in needed:
                cnt[o.eng] += 1
                o.sig = ("eng:" + o.eng, cnt[o.eng])
        seen = {e: {} for e in self.ENGS}
        for o in ops:
            req = {}
            for d in o.deps:
                do = ops[d]
                if (not do.is_dma) and do.eng == "pe" and o.eng == "pe" and not o.is_dma:
                    continue
                key, val = do.sig
                if do.is_dma and do.lane in self.lane_waitall:
                    val = 16 * self.lane_counts[do.lane]
                if val > req.get(key, 0):
                    req[key] = val
            s = seen[o.eng]
            for key, val in req.items():
                if s.get(key, 0) >= val:
                    continue
                s[key] = val
                o.waits.append((key, val))

    def emit(self, nc, sems, block, final_waits):
        per = {e: [o for o in self.ops if o.eng == e] for e in self.ENGS}

        def run(e, name):
            for o in per[name]:
                for key, val in o.waits:
                    e.wait_ge(sems[key], val)
                ins = o.fn(e)
                if o.sig is not None:
                    ins.then_inc(sems[o.sig[0]], 16 if o.is_dma else 1)
            if name == "sp":
                for key, val in final_waits:
                    e.wait_ge(sems[key], val)

        @block.sync
        def _(e):
            run(e, "sp")

        @block.scalar
        def _(e):
            run(e, "act")

        @block.gpsimd
        def _(e):
            run(e, "pool")

        @block.vector
        def _(e):
            run(e, "dve")

        @block.tensor
        def _(e):
            run(e, "pe")


def build_nc():
    nc = bass.Bass("TRN2", target_bir_lowering=False)
    P = Prog()

    def din(name, shape, dt=F32):
        return nc.dram_tensor(name, list(shape), dt, kind="ExternalInput").ap()

    xin = din("x", [TOK, D])
    w_in_p = din("w_in_p", [D, NCOL])
    w_out_d = din("w_out", [D, D])
    w_up_d = din("w_up", [D, DFF])
    w_dn_d = din("w_down", [DFF, D])
    w_pool_d = din("w_pool", [512, 128])
    g1_d = din("g1", [1, D])
    g2_d = din("g2", [1, D])
    gf_d = din("gf", [1, D])
    psc_d = din("pscale", [128, 4])
    sink_d = din("sinks", [1, 8])
    cos_d = din("cosT", [128, SEQ])
    sin_d = din("sinT", [128, SEQ])
    rot_d = din("rotm", [128, 128])
    idn_d = din("ident", [128, 128])
    msk_d = din("maskcn", [128, 256])
    icn_d = din("invcnt", [1, 64])
    yout = nc.dram_tensor("y", [TOK, D], F32, kind="ExternalOutput").ap()
    wup_s = nc.dram_tensor("wup_s", [D, DFF], BF16, kind="Internal").ap()
    wdn_s = nc.dram_tensor("wdn_s", [DFF, D], BF16, kind="Internal").ap()

    import contextlib
    es_ = contextlib.ExitStack()
    with es_:
        def sb(name, shape, dt):
            return es_.enter_context(nc.sbuf_tensor(name, list(shape), dt))

        LU = 16 + T
        wcat = sb("wcat", [128, 8, NCOL], BF16)
        wout = sb("wout", [128, 8, D], BF16)
        wpool = sb("wpool", [128, 4, 128], BF16)
        wu = [sb(f"wu{i}", [128, 8, 512], BF16) for i in range(NSLOT)]
        wd = [sb(f"wd{i}", [128, 4, D], BF16) for i in range(NSLOT)]
        xs = [sb(f"xres{i}", [128, NB, D], F32) for i in range(2)]
        hT1 = sb("hT1", [128, 8, T], BF16)
        hT2s = [sb(f"hT2_{i}", [128, 8, T], BF16) for i in range(2 if OPT_HT2X2 else 1)]
        junkH = sb("junkH", [128, D], BF16) if OPT_JUNKH else None
        hbuf = [sb(f"hbuf{i}", [128, D], BF16) for i in range(2)]
        qT = sb("qT", [128, 4, T], BF16)
        kT = [sb(f"kT{i}", [128, 2, T], BF16) for i in range(2)]
        vaug = [sb(f"vaug{i}", [128, NB, 2, 128], BF16) for i in range(2)]
        ub1 = sb("ub", [128, LU], F32)
        ub = [ub1, ub1]
        xA = sb("xA", [128, D], F32)
        hsave = sb("hsave", [128, 4, 16], F32)
        tmpA = sb("tmpA", [128, LU], F32)
        tmpB = sb("tmpB", [128, LU], F32)
        dT = [sb(f"dT{i}", [128, T], BF16) for i in range(2)]
        yT = sb("yT", [128, 4, T], BF16)
        attnT = sb("attnT", [128, 4, T], BF16)
        PT = [[sb(f"PT{i}{k}", [128, 4, 256], BF16) for k in range(2)] for i in range(2)]
        if OPT_AT1:
            aT0_ = sb("aT0", [128, 8, T], BF16)
            aT = [aT0_, aT0_]
        else:
            aT = [sb(f"aT{i}", [128, 8, T], BF16) for i in range(2)]
        cosb = sb("cosb", [128, T], F32)
        sinb = sb("sinb", [128, T], F32)
        g1b = sb("g1b", [128, D], BF16)
        g2b = sb("g2b", [128, D], BF16)
        gfb = sb("gfb", [128, D], BF16)
        maskt = sb("maskt", [128, 256], BF16)
        rotm = sb("rotm_s", [128, 128], BF16)
        ident = sb("ident_s", [128, 128], BF16)
        sinkt = sb("sinkt", [128, 8], F32)
        est = sb("est", [128, 8], F32)
        psct = sb("psct", [128, 4], F32)
        icnt = sb("icnt", [128, 64], F32)
        st = sb("stats", [128, 48], F32)
        fix = sb("fixt", [128, 16], F32)
        xb = dT[0]
        ropeA, ropeB = tmpA, tmpB
        sA, sB = tmpA, tmpB

        banks = [es_.enter_context(nc.psum_tensor(f"bank{i}", [128, 512], F32)) for i in range(8)]
        UB = (0, 1)
        DB = (2, 3)
        MB = (4, 5, 6, 7)

        PL = "prolog"
        PW = "prolog_w"
        for kc in range(8):
            P.op("pool", lambda e, kc=kc: e.dma_start(out=wcat[:, kc, :], in_=w_in_p[kc * 128:(kc + 1) * 128, :]),
                 writes=[("wcat",)], lane=PL, waitall=True, nbytes=128 * NCOL * 4)
        for kc in range(8):
            P.op("pool", lambda e, kc=kc: e.dma_start(out=wout[:, kc, :], in_=w_out_d[kc * 128:(kc + 1) * 128, :]),
                 writes=[("wout",)], lane=PL, waitall=True, nbytes=128 * D * 4)
        for g in range(4):
            P.op("pool", lambda e, g=g: e.dma_start(out=wpool[:, g, :], in_=w_pool_d[g * 128:(g + 1) * 128, :]),
                 writes=[("wpool",)], lane=PL, waitall=True, nbytes=65536)
        P.op("pool", lambda e: e.dma_start(out=rotm[:], in_=rot_d[:, :]), writes=[("consts",)], lane=PL, waitall=True, nbytes=65536)
        P.op("pool", lambda e: e.dma_start(out=ident[:], in_=idn_d[:, :]), writes=[("consts",)], lane=PL, waitall=True, nbytes=65536)
        P.op("pool", lambda e: e.dma_start(out=maskt[:], in_=msk_d[:, :]), writes=[("consts",)], lane=PL, waitall=True, nbytes=131072)
        P.op("pool", lambda e: e.dma_start(out=g1b[:], in_=g1_d.partition_broadcast(128)), writes=[("g1b",)], lane=PL, waitall=True, nbytes=4096)
        P.op("pool", lambda e: e.dma_start(out=g2b[:], in_=g2_d.partition_broadcast(128)), writes=[("g2b",)], lane=PL, waitall=True, nbytes=4096)
        P.op("pool", lambda e: e.dma_start(out=gfb[:], in_=gf_d.partition_broadcast(128)), writes=[("gfb",)], lane=PL, waitall=True, nbytes=4096)
        for i in range(8):
            P.op("pool", lambda e, i=i: e.dma_start(out=wup_s[i * 128:(i + 1) * 128, :], in_=w_up_d[i * 128:(i + 1) * 128, :]),
                 writes=[("wup_s",)], lane=PW, waitall=True, nbytes=128 * DFF * 4)
        for i in range(8):
            P.op("pool", lambda e, i=i: e.dma_start(out=wdn_s[i * 512:(i + 1) * 512, :], in_=w_dn_d[i * 512:(i + 1) * 512, :]),
                 writes=[("wdn_s",)], lane=PW, waitall=True, nbytes=512 * D * 4)
        PS = "prolog_sp"
        P.op("sp", lambda e: e.dma_start(out=sinkt[:], in_=sink_d.partition_broadcast(128)), writes=[("sinkt",)], lane=PS, waitall=True, nbytes=4096)
        P.op("sp", lambda e: e.dma_start(out=psct[:], in_=psc_d[:, :]), writes=[("psct",)], lane=PS, waitall=True, nbytes=2048)
        P.op("sp", lambda e: e.dma_start(out=icnt[:], in_=icn_d.partition_broadcast(128)), writes=[("icnt",)], lane=PS, waitall=True, nbytes=32768)
        P.op("act", lambda e: e.activation(out=est[:], in_=sinkt[:], func=AF.Exp), reads=[("sinkt",)], writes=[("est",)], n=8)
        for i in range(2):
            P.op("pool", lambda e, i=i: e.memset(vaug[i][:], 1.0), writes=[("vaug", i, b) for b in range(NB)], n=1024)

        def rstd_ops(xt, xk, b, junk, junk_key, sc, src=None):
            c_ss, c_ln, c_rs = sc + b, sc + 4 + b, sc + 8 + b
            if src is None:
                src = xt[:, b, :]
            P.op("act", lambda e: e.activation(out=junk, in_=src, func=AF.Square, accum_out=st[:, c_ss:c_ss + 1]),
                 reads=[(xk, b)], writes=(junk_key if isinstance(junk_key, list) else [junk_key]) + [("st", c_ss)], n=D)
            P.op("act", lambda e: e.activation(out=st[:, c_ln:c_ln + 1], in_=st[:, c_ss:c_ss + 1], func=AF.Ln,
                                               scale=1.0 / D, bias=EPS),
                 reads=[("st", c_ss)], writes=[("st", c_ln)], n=1)
            P.op("act", lambda e: e.activation(out=st[:, c_rs:c_rs + 1], in_=st[:, c_ln:c_ln + 1], func=AF.Exp, scale=-0.5),
                 reads=[("st", c_ln)], writes=[("st", c_rs)], n=1)
            return c_rs

        def norm_transpose_phase(xt, xk, gb, gkey, sc, tbanks, hT, hk, stage_rows=None, hbufs=None, hkeys=None):
            for b in range(NB):
                if hbufs is None:
                    hb, hbk = hbuf[b % 2], [("hbuf", b % 2)]
                else:
                    hb, hbk = hbufs[b % 2], hkeys[b % 2]
                if stage_rows is not None:
                    rr = stage_rows + b * 128
                    P.op("sp", lambda e, rr=rr: e.dma_start(out=xA[:], in_=xin[rr:rr + 128, :]),
                         writes=[("xA", bb) for bb in range(NB)], lane="xa", nbytes=1 << 19)
                    xsrc = xA[:]
                    c_rs = rstd_ops(None, xk, b, hb[:], hbk, sc, src=xsrc)
                else:
                    xsrc = xt[:, b, :]
                    c_rs = rstd_ops(xt, xk, b, hb[:], hbk, sc)
                P.op("dve", lambda e, xsrc=xsrc, hb=hb, c_rs=c_rs: e.scalar_tensor_tensor(
                    out=hb[:], in0=xsrc, scalar=st[:, c_rs:c_rs + 1], in1=gb[:], op0=ALU.mult, op1=ALU.mult),
                    reads=[(xk, b), ("st", c_rs), gkey], writes=hbk, n=D)
                bk = tbanks[b % 2]
                pst = banks[bk][:].bitcast(BF16)

                def tr(e, hb=hb, pst=pst):
                    ins = None
                    for kc in range(8):
                        ins = e.transpose(out=pst[:, kc * 128:(kc + 1) * 128], in_=hb[:, kc * 128:(kc + 1) * 128], identity=ident[:])
                    return ins
                P.op("pe", tr, reads=hbk + [("consts",)], writes=[("ps", bk)], n=1024, k=8)
                P.op("act", lambda e, b=b, pst=pst: e.activation(
                    out=hT[:, :, b * 128:(b + 1) * 128], in_=pst.rearrange("p (k t) -> p k t", k=8), func=AF.Copy),
                    reads=[("ps", bk)], writes=[(hk, b)], n=1024)

        def mm_group(out_ap, pairs, reads, bk):
            def f(e):
                ins = None
                n = len(pairs)
                for i, (l, r, _) in enumerate(pairs):
                    ins = e.matmul(out_ap, lhsT=l, rhs=r, start=(i == 0), stop=(i == n - 1))
                return ins
            P.op("pe", f, reads=reads, writes=[("ps", bk)], n=sum(p[2] for p in pairs), k=len(pairs))

        H1_ALL = [("hT1", b) for b in range(NB)]
        NPT = NT * NPIECE

        def load_wu(gp):
            slot, pc = gp % NSLOT, gp % NPIECE
            P.op("sp", lambda e: e.dma_start(
                out=wu[slot][:], in_=wup_s.rearrange("(kc p) f -> p kc f", p=128)[:, :, pc * 512:(pc + 1) * 512]),
                reads=[("wup_s",)], writes=[("wu", slot)], lane=f"wu{slot}", nbytes=1 << 20)

        def load_wd(gp):
            slot, pc = gp % NSLOT, gp % NPIECE
            P.op("sp", lambda e: e.dma_start(
                out=wd[slot][:], in_=wdn_s[pc * 512:(pc + 1) * 512, :].rearrange("(fc p) d -> p fc d", p=128)),
                reads=[("wdn_s",)], writes=[("wd", slot)], lane=f"wd{slot}", nbytes=1 << 20)

        def mixer(ti):
            tpos = ti % TPS
            par = ti % 2
            r0 = ti * T
            t0 = tpos * T
            x = xs[par]
            xk = "x%d" % par
            P.cur_tag = (ti, 'L')
            for b in range(NB):
                P.op("sp", lambda e, b=b: e.dma_start(out=x[:, b, :], in_=xin[r0 + b * 128: r0 + (b + 1) * 128, :]),
                     writes=[(xk, b)], lane=f"xld{par}{b}", nbytes=1 << 19)
            P.op("sp", lambda e: e.dma_start(out=cosb[:], in_=cos_d[:, t0:t0 + T]), writes=[("cos",)], lane="cos", nbytes=1 << 18)
            P.op("sp", lambda e: e.dma_start(out=sinb[:], in_=sin_d[:, t0:t0 + T]), writes=[("sin",)], lane="sin", nbytes=1 << 18)
            if ti == 0:
                for gp in range(NSLOT):
                    load_wu(gp)
                    load_wd(gp)
            if DBG_STOP == 'prolog':
                return False
            P.cur_tag = (ti, 'A')
            if OPT_XA and ti >= OPT_XAFROM:
                norm_transpose_phase(None, "xA", g1b, ("g1b",), 0, (MB[0], MB[1]), hT1, "hT1", stage_rows=r0)
            else:
                norm_transpose_phase(x, xk, g1b, ("g1b",), 0, (MB[0], MB[1]), hT1, "hT1")
            if DBG_STOP == 'A':
                return False

            P.cur_tag = (ti, 'B')
            pbank = (MB[0], MB[1])
            rbank = (MB[2], MB[3])
            chunk_list = [("q", c) for c in range(4)] + [("k", kv) for kv in range(2)]
            for ci, (kind, j) in enumerate(chunk_list):
                col0 = (QC0 if kind == "q" else KC0) + j * 128
                bk = pbank[ci % 2]
                rb = rbank[ci % 2]
                pairs = [(wcat[:, kc, col0:col0 + 128], hT1[:, kc, :], T) for kc in range(8)]
                mm_group(banks[bk][:], pairs, [("wcat",)] + H1_ALL, bk)
                P.op("act", lambda e, bk=bk: e.activation(out=xb[:], in_=banks[bk][:], func=AF.Copy),
                     reads=[("ps", bk)], writes=[("dT", 0)], n=T)
                mm_group(banks[rb][:], [(rotm[:], xb[:], T)], [("dT", 0), ("consts",)], rb)
                P.op("dve", lambda e, bk=bk: e.tensor_tensor(out=ropeA[:, 0:T], in0=banks[bk][:], in1=cosb[:], op=ALU.mult),
                     reads=[("ps", bk), ("cos",)], writes=[("tmpA",)], n=T)
                P.op("dve", lambda e, rb=rb: e.tensor_tensor(out=ropeB[:, 0:T], in0=banks[rb][:], in1=sinb[:], op=ALU.mult),
                     reads=[("ps", rb), ("sin",)], writes=[("tmpB",)], n=T)
                if kind == "q":
                    dst, wkey = qT[:, j, :], [("qT", j)]
                else:
                    dst, wkey = kT[par][:, j, :], [("kT", par, j)]
                P.op("pool", lambda e, dst=dst: e.tensor_tensor(out=dst, in0=ropeA[:, 0:T], in1=ropeB[:, 0:T], op=ALU.add),
                     reads=[("tmpA",), ("tmpB",)], writes=wkey, n=T)
            vb = MB[0]

            def vmm(e):
                ins = None
                for b in range(NB):
                    for kc in range(8):
                        ins = e.matmul(banks[vb][:, b * 128:(b + 1) * 128], lhsT=hT1[:, kc, b * 128:(b + 1) * 128],
                                       rhs=wcat[:, kc, VC0:VC0 + 128], start=(kc == 0), stop=(kc == 7))
                return ins
            P.op("pe", vmm, reads=[("wcat",)] + H1_ALL, writes=[("ps", vb)], n=32 * 128, k=32)
            P.op("act", lambda e: e.activation(
                out=vaug[par][:, :, :, 0:64], in_=banks[vb][:].rearrange("p (b k d) -> p b k d", b=NB, k=2), func=AF.Copy),
                reads=[("ps", vb)], writes=[("vaug", par, b) for b in range(NB)], n=T)
            if DBG_STOP == 'B':
                return False

            P.cur_tag = (ti, 'C')
            for g in range(4):
                w = POOL_W[g]
                L = LU
                uu = ub[g % 2]
                ukey = ("ub", 0)
                bk = MB[g % 2]
                yb = MB[2 + (g % 2)]
                col0 = UC0 + g * 128
                pairs = [(wcat[:, kc, col0:col0 + 128], hT1[:, kc, :], T) for kc in range(8)]
                mm_group(banks[bk][:], pairs, [("wcat",)] + H1_ALL, bk)
                if tpos == 0:
                    P.op("pool", lambda e, uu=uu: e.memset(uu[:, 0:16], 0.0), writes=[ukey], n=16)
                else:
                    P.op("pool", lambda e, uu=uu, g=g: e.tensor_copy(out=uu[:, 0:16], in_=hsave[:, g, :]),
                         reads=[("hsave", g)], writes=[ukey], n=16)
                P.op("act", lambda e, bk=bk, uu=uu, w=w: e.activation(out=uu[:, 16:L], in_=banks[bk][:], func=AF.Copy, scale=1.0 / w),
                     reads=[("ps", bk)], writes=[ukey], n=T)
                P.op("pool", lambda e, uu=uu, g=g: e.tensor_copy(out=hsave[:, g, :], in_=uu[:, T:T + 16]),
                     reads=[ukey], writes=[("hsave", g)], n=16)
                P.op("pool", lambda e, uu=uu: e.tensor_tensor(out=sA[:, 1:L], in0=uu[:, 1:L], in1=uu[:, 0:L - 1], op=ALU.add),
                     reads=[ukey], writes=[("tmpA",)], n=L)
                S_t, S_key = sA, ("tmpA",)
                if w >= 4:
                    P.op("pool", lambda e: e.tensor_tensor(out=sB[:, 3:L], in0=sA[:, 3:L], in1=sA[:, 1:L - 2], op=ALU.add),
                         reads=[("tmpA",)], writes=[("tmpB",)], n=L)
                    S_t, S_key = sB, ("tmpB",)
                if w >= 8:
                    P.op("pool", lambda e: e.tensor_tensor(out=sA[:, 7:L], in0=sB[:, 7:L], in1=sB[:, 3:L - 4], op=ALU.add),
                         reads=[("tmpB",)], writes=[("tmpA",)], n=L)
                    S_t, S_key = sA, ("tmpA",)
                if w >= 16:
                    P.op("pool", lambda e: e.tensor_tensor(out=sB[:, 15:L], in0=sA[:, 15:L], in1=sA[:, 7:L - 8], op=ALU.add),
                         reads=[("tmpA",)], writes=[("tmpB",)], n=L)
                    S_t, S_key = sB, ("tmpB",)
                dd = dT[g % 2]
                P.op("dve", lambda e, S_t=S_t, dd=dd, bk=bk: e.tensor_tensor(
                    out=dd[:], in0=S_t[:, 16:L], in1=banks[bk][:], op=ALU.subtract),
                    reads=[S_key, ("ps", bk)], writes=[("dT", g % 2)], n=T)
                if tpos == 0:
                    P.op("pool", lambda e, g=g, S_t=S_t: e.tensor_tensor(
                        out=fix[:], in0=S_t[:, 16:32], in1=icnt[:, g * 16:(g + 1) * 16], op=ALU.mult),
                        reads=[S_key, ("icnt",)], writes=[("fix",)], n=16)
                    P.op("dve", lambda e, dd=dd, bk=bk: e.tensor_tensor(out=dd[:, 0:16], in0=fix[:], in1=banks[bk][:, 0:16], op=ALU.subtract),
                         reads=[("fix",), ("ps", bk)], writes=[("dT", g % 2)], n=16)
                mm_group(banks[yb][:], [(wpool[:, g, :], dd[:], T)], [("wpool",), ("dT", g % 2)], yb)
                P.op("act", lambda e, g=g, yb=yb: e.activation(out=yT[:, g, :], in_=banks[yb][:], func=AF.Copy, scale=psct[:, g:g + 1]),
                     reads=[("ps", yb), ("psct",)], writes=[("yT", g)], n=T)
            if DBG_STOP == 'C':
                return False

            P.cur_tag = (ti, 'D')
            dent, denr = tmpA, tmpB
            groups = ([-1] if tpos > 0 else []) + [0, 1, 2, 3]
            for kb in groups:
                gpar = kb % 2
                if kb == -1:
                    ksrc, kcol, kpar = kT[1 - par], 3 * 128, 1 - par
                    q0, nq, pcol = 0, 128, 128
                else:
                    ksrc, kcol, kpar = kT[par], kb * 128, par
                    q0 = kb * 128
                    nq = 256 if kb < 3 else 128
                    pcol = 0
                for kv in range(2):
                    ptile = PT[gpar][kv]
                    bX, bY = MB[0], MB[1]

                    def smm(e, ksrc=ksrc, kcol=kcol, kv=kv, q0=q0, nq=nq):
                        ins = None
                        for hp in range(2):
                            c = kv * 2 + hp
                            for half in range(2):
                                bk = bX if half == 0 else bY
                                ins = e.matmul(banks[bk][:, hp * 256: hp * 256 + nq],
                                               lhsT=ksrc[half * 64:(half + 1) * 64, kv, kcol:kcol + 128],
                                               rhs=qT[half * 64:(half + 1) * 64, c, q0:q0 + nq], start=True, stop=True)
                        return ins
                    P.op("pe", smm, reads=[("kT", kpar, kv), ("qT", kv * 2), ("qT", kv * 2 + 1)],
                         writes=[("ps", bX), ("ps", bY)], n=2 * nq, k=4)
                    for half, bk in ((0, bX), (1, bY)):
                        P.op("act", lambda e, bk=bk, ptile=ptile, half=half, nq=nq, pcol=pcol: e.activation(
                            out=ptile[:, half::2, pcol:pcol + nq],
                            in_=banks[bk][:].rearrange("p (h q) -> p h q", h=2)[:, :, 0:nq], func=AF.Exp, scale=0.125),
                            reads=[("ps", bk)], writes=[("PT", gpar, kv, half)], n=2 * nq)
                    P.op("pool", lambda e, ptile=ptile, nq=nq, pcol=pcol: e.tensor_tensor(
                        out=ptile[:, :, pcol:pcol + nq], in0=ptile[:, :, pcol:pcol + nq],
                        in1=maskt[:, pcol:pcol + nq].unsqueeze(1).broadcast_to([128, 4, nq]), op=ALU.mult),
                        reads=[("PT", gpar, kv, 0), ("PT", gpar, kv, 1), ("consts",)],
                        writes=[("PT", gpar, kv, 0), ("PT", gpar, kv, 1)], n=4 * nq)
                if kb < 0:
                    continue
                b = kb
                has_prev = not (tpos == 0 and kb == 0)
                for kv in range(2):
                    ob = MB[2 + kv]
                    pairs = []
                    rd = [("PT", gpar, kv, 0), ("PT", gpar, kv, 1), ("vaug", par, b)]
                    if has_prev:
                        if kb == 0:
                            pv, pblk, ppar = vaug[1 - par], 3, 1 - par
                        else:
                            pv, pblk, ppar = vaug[par], kb - 1, par
                        pairs.append((pv[:, pblk, kv, :], PT[1 - gpar][kv][:, :, 128:256], 512))
                        rd += [("PT", 1 - gpar, kv, 0), ("PT", 1 - gpar, kv, 1), ("vaug", ppar, pblk)]
                    pairs.append((vaug[par][:, b, kv, :], PT[gpar][kv][:, :, 0:128], 512))
                    O3 = banks[ob][:].rearrange("p (h q) -> p h q", h=4)
                    mm_group(O3, pairs, rd, ob)
                    P.op("dve", lambda e, O3=O3, kv=kv: e.tensor_tensor(
                        out=dent[64:128, 0:512].rearrange("p (h q) -> p h q", h=4), in0=O3[64:128, :, :],
                        in1=est[64:128, kv * 4:(kv + 1) * 4].unsqueeze(2).broadcast_to([64, 4, 128]), op=ALU.add),
                        reads=[("ps", ob), ("est",)], writes=[("tmpA",)], n=512)
                    P.op("act", lambda e: e.activation(out=dent[64:128, 0:512], in_=dent[64:128, 0:512], func=AF.Ln),
                         reads=[("tmpA",)], writes=[("tmpA",)], n=512)
                    P.op("act", lambda e: e.activation(out=denr[0:64, 0:512], in_=dent[64:128, 0:512], func=AF.Exp, scale=-1.0),
                         reads=[("tmpA",)], writes=[("tmpB",)], n=512)
                    R3 = denr[:, 0:512].rearrange("p (h q) -> p h q", h=4)
                    for odd in range(2):
                        P.op("dve", lambda e, O3=O3, odd=odd, kv=kv, b=b, R3=R3: e.tensor_tensor(
                            out=attnT[odd * 64:(odd + 1) * 64, kv * 2:kv * 2 + 2, b * 128:(b + 1) * 128],
                            in0=O3[0:64, odd::2, :], in1=R3[0:64, odd::2, :], op=ALU.mult),
                            reads=[("ps", ob), ("tmpB",)], writes=[("attnT", kv, b, odd)], n=256)
            if DBG_STOP == 'D':
                return False

            P.cur_tag = (ti, 'E')
            oi = 0
            for b in range(NB):
                for half in range(2):
                    bk = MB[oi % 4]
                    oi += 1
                    pairs = []
                    for c in range(4):
                        pairs.append((attnT[:, c, b * 128:(b + 1) * 128], wout[:, c, half * 512:(half + 1) * 512], 512))
                    for g in range(4):
                        pairs.append((yT[:, g, b * 128:(b + 1) * 128], wout[:, 4 + g, half * 512:(half + 1) * 512], 512))
                    rd = [("wout",)] + [("attnT", kv, b, odd) for kv in range(2) for odd in range(2)] + [("yT", g) for g in range(4)]
                    mm_group(banks[bk][:], pairs, rd, bk)
                    P.op("dve", lambda e, b=b, half=half, bk=bk: e.tensor_tensor(
                        out=x[:, b, half * 512:(half + 1) * 512], in0=banks[bk][:], in1=x[:, b, half * 512:(half + 1) * 512], op=ALU.add),
                        reads=[("ps", bk), (xk, b)], writes=[(xk, b)], n=512)
            if DBG_STOP == 'E':
                return False
            return True

        def mlp(ti):
            par = ti % 2
            hT2 = hT2s[par % len(hT2s)]
            h2k = "hT2_%d" % (par % len(hT2s))
            H2_ALL = [(h2k, b) for b in range(NB)]
            r0 = ti * T
            x = xs[par]
            xk = "x%d" % par
            P.cur_tag = (ti, 'F')
            yflat = yT[:].rearrange("p g t -> p (g t)")
            norm_transpose_phase(x, xk, g2b, ("g2b",), 12, (DB[0], DB[1]) if OPT_FBANKS == 'DB' else (MB[2], MB[3]), hT2, h2k,
                                 hbufs=[yflat[:, 0:D], yflat[:, D:2 * D]] if OPT_FY else None,
                                 hkeys=[[("yT", 0), ("yT", 1)], [("yT", 2), ("yT", 3)]] if OPT_FY else None)
            if DBG_STOP == 'F':
                return False
            P.cur_tag = (ti, 'G')
            ui = 0
            di = 0
            for ga in range(NPIECE // 2):
                abuf = aT[ga % 2]
                slots = []
                gp0 = ti * NPIECE + ga * 2
                for pp in range(2):
                    gp = gp0 + pp
                    slot = gp % NSLOT
                    slots.append(slot)
                    for f in range(4):
                        fi = pp * 4 + f
                        bk = UB[ui % 2]
                        ui += 1
                        pairs = [(wu[slot][:, kc, f * 128:(f + 1) * 128], hT2[:, kc, :], T) for kc in range(8)]
                        mm_group(banks[bk][:], pairs, [("wu", slot)] + H2_ALL, bk)
                        P.op("act", lambda e, bk=bk: e.activation(out=banks[bk][:], in_=banks[bk][:], func=AF.Relu),
                             reads=[("ps", bk)], writes=[("ps", bk)], n=T)
                        P.op("act", lambda e, bk=bk, abuf=abuf, fi=fi: e.activation(out=abuf[:, fi, :], in_=banks[bk][:], func=AF.Square),
                             reads=[("ps", bk)], writes=[("aT", (0 if OPT_AT1 else ga % 2), fi)], n=T)
                    if gp + NSLOT < NPT:
                        load_wu(gp + NSLOT)
                for b in range(NB):
                    for half in range(2):
                        bk = DB[di % 2]
                        di += 1
                        pairs = []
                        for fi in range(8):
                            pairs.append((abuf[:, fi, b * 128:(b + 1) * 128], wd[slots[fi // 4]][:, fi % 4, half * 512:(half + 1) * 512], 512))
                        rd = [("wd", s_) for s_ in slots] + [("aT", (0 if OPT_AT1 else ga % 2), fi) for fi in range(8)]
                        mm_group(banks[bk][:], pairs, rd, bk)
                        P.op("dve", lambda e, b=b, half=half, bk=bk: e.tensor_tensor(
                            out=x[:, b, half * 512:(half + 1) * 512], in0=banks[bk][:], in1=x[:, b, half * 512:(half + 1) * 512], op=ALU.add),
                            reads=[("ps", bk), (xk, b)], writes=[(xk, b)], n=512)
                for gq in (gp0 + NSLOT, gp0 + NSLOT + 1):
                    if gq < NPT:
                        load_wd(gq)
            if DBG_STOP == 'G':
                return False
            P.cur_tag = (ti, 'H')
            for b in range(NB):
                if OPT_JUNKH:
                    junk, jkey = junkH[:], ("junkH",)
                else:
                    junk, jkey = aT[0][:, 0:2, :], ("aT", 0, 0)
                if OPT_HSQ_DVE:
                    c_ss, c_ln, c_rs = 24 + b, 28 + b, 32 + b
                    P.op("dve", lambda e, b=b, junk=junk, c_ss=c_ss: e.scalar_tensor_tensor(
                        out=junk, in0=x[:, b, :], scalar=1.0, in1=x[:, b, :], op0=ALU.mult, op1=ALU.mult, accum_out=st[:, c_ss:c_ss + 1]),
                        reads=[(xk, b)], writes=[jkey, ("st", c_ss)], n=D)
                    P.op("act", lambda e, c_ss=c_ss, c_ln=c_ln: e.activation(out=st[:, c_ln:c_ln + 1], in_=st[:, c_ss:c_ss + 1], func=AF.Ln,
                                                                         scale=1.0 / D, bias=EPS),
                         reads=[("st", c_ss)], writes=[("st", c_ln)], n=1)
                    P.op("act", lambda e, c_ln=c_ln, c_rs=c_rs: e.activation(out=st[:, c_rs:c_rs + 1], in_=st[:, c_ln:c_ln + 1], func=AF.Exp, scale=-0.5),
                         reads=[("st", c_ln)], writes=[("st", c_rs)], n=1)
                else:
                    c_rs = rstd_ops(x, xk, b, junk, jkey, 24)
                P.op("act", lambda e, b=b, c_rs=c_rs: e.activation(
                    out=x[:, b, :], in_=x[:, b, :], func=AF.Copy, scale=st[:, c_rs:c_rs + 1]),
                    reads=[(xk, b), ("st", c_rs)], writes=[(xk, b)], n=D)
                P.op("pool", lambda e, b=b: e.tensor_tensor(out=x[:, b, :], in0=x[:, b, :], in1=gfb[:], op=ALU.mult),
                     reads=[(xk, b), ("gfb",)], writes=[(xk, b)], n=D)
                P.op("sp", lambda e, b=b: e.dma_start(out=yout[r0 + b * 128: r0 + (b + 1) * 128, :], in_=x[:, b, :]),
                     reads=[(xk, b)], lane=f"st{par}{b}", nbytes=1 << 19)
            return True

        for ti in range(NT):
            if not mixer(ti):
                break
            if not mlp(ti):
                break

        if DO_SCHEDULE:
            P.schedule()
        P.finalize()
        final_waits = []
        for l, c in P.lane_counts.items():
            if l.startswith("st") or DBG_STOP is not None:
                final_waits.append(("lane:" + l, 16 * c))

        sem_names = ["eng:" + e for e in Prog.ENGS] + ["lane:" + l for l in P.lane_counts]
        sems = {}
        for i, n in enumerate(sem_names):
            sems[n] = es_.enter_context(nc.semaphore(f"s{i}"))
        block = es_.enter_context(nc.Block())
        P.emit(nc, sems, block, final_waits)
    return nc


def _consts():
    half = 32
    inv_freq = 10000.0 ** (-np.arange(half, dtype=np.float64) / float(half))
    pos = np.arange(SEQ, dtype=np.float64)
    ang = pos[:, None] * inv_freq[None, :]
    cosv = np.cos(ang).astype(np.float32)
    sinv = np.sin(ang).astype(np.float32)
    fidx = (np.arange(128) % 64) % 32
    cosT = np.ascontiguousarray(cosv[:, fidx].T)
    sinT = np.ascontiguousarray(sinv[:, fidx].T)
    rot = np.zeros((128, 128), np.float32)
    for m in range(128):
        base = (m // 64) * 64
        d = m % 64
        if d < 32:
            rot[base + d + 32, m] = -1.0
        else:
            rot[base + d - 32, m] = 1.0
    ident = np.eye(128, dtype=np.float32)
    j = np.arange(128)[:, None]
    q = np.arange(128)[None, :]
    mask = np.concatenate([(j <= q), (j > q)], axis=1).astype(np.float32)
    t = np.arange(16)
    icn = np.concatenate([w / np.minimum(t + 1, w) for w in POOL_W]).astype(np.float32)[None, :]
    return cosT, sinT, rot, ident, mask, icn


_NC_CACHE = {}


def kernel(x, attn_norm_g, w_in, attn_sinks, w_pool, pool_scale, w_out,
           mlp_norm_g, w_up, w_down, final_norm_g):
    x = np.asarray(x, dtype=np.float32)
    w_in0 = np.asarray(w_in, dtype=np.float32)[0]
    cols = np.concatenate([
        np.arange(0, 512),
        np.arange(512, 576), np.arange(512, 576),
        np.arange(576, 640), np.arange(576, 640),
        np.arange(768, 1280),
        np.arange(640, 768),
    ])
    w_in_p = np.ascontiguousarray(w_in0[:, cols])
    cosT, sinT, rot, ident, mask, icn = _consts()
    shared = {
        "w_in_p": w_in_p,
        "w_out": np.ascontiguousarray(np.asarray(w_out, np.float32)[0]),
        "w_up": np.ascontiguousarray(np.asarray(w_up, np.float32)[0]),
        "w_down": np.ascontiguousarray(np.asarray(w_down, np.float32)[0]),
        "w_pool": np.ascontiguousarray(np.asarray(w_pool, np.float32)[0].reshape(512, 128)),
        "g1": np.ascontiguousarray(np.asarray(attn_norm_g, np.float32).reshape(1, D)),
        "g2": np.ascontiguousarray(np.asarray(mlp_norm_g, np.float32).reshape(1, D)),
        "gf": np.ascontiguousarray(np.asarray(final_norm_g, np.float32).reshape(1, D)),
        "pscale": np.ascontiguousarray(np.asarray(pool_scale, np.float32)[0].reshape(4, 128).T),
        "sinks": np.ascontiguousarray(np.asarray(attn_sinks, np.float32).reshape(1, 8)),
        "cosT": cosT, "sinT": sinT, "rotm": rot, "ident": ident, "maskcn": mask, "invcnt": icn,
    }
    in_maps = []
    for c in range(NCORES):
        m = dict(shared)
        m["x"] = np.ascontiguousarray(x[c * SEQ_PER_CORE:(c + 1) * SEQ_PER_CORE].reshape(TOK, D))
        in_maps.append(m)
    if "nc" not in _NC_CACHE:
        _NC_CACHE["nc"] = build_nc()
    nc = _NC_CACHE["nc"]
    res = run_bass_kernel_spmd(nc, in_maps, core_ids=list(range(NCORES)))
    out = np.concatenate([np.asarray(r["y"]).reshape(SEQ_PER_CORE, SEQ, D) for r in res.results], axis=0)
    return out.astype(np.float32, copy=False)
```

```python
import numpy as np
import concourse.bass as bass
import concourse.mybir as mybir
from concourse.bass_utils import run_bass_kernel_spmd

F32 = mybir.dt.float32
BF16 = mybir.dt.bfloat16
ALU = mybir.AluOpType
AF = mybir.ActivationFunctionType

NCORES = 8
D = 1024
SEQ = 2048
T = 512
NB = 4
SEQ_PER_CORE = 4
TOK = SEQ_PER_CORE * SEQ
NT = TOK // T
TPS = SEQ // T
NCOL = 1408
DFF = 4096
EPS = 1e-6
POOL_W = (2, 4, 8, 16)
NPIECE = 8
NSLOT = 3
DBG_STOP = None
DBG_OPLIMIT = None
DBG_PRINT = False
DO_SCHEDULE = True
OPT_JUNKH = False
OPT_PRIO1 = ('G', 'H')
OPT_FBANKS = 'DB'
OPT_LAT = 500.0
OPT_TILEPRIO = True
OPT_MIXLEAD = 1
OPT_XA = True
OPT_FY = True
OPT_FTOP = True
OPT_XAFROM = 2
OPT_LEADFROM = 2
OPT_MIXLEAD0 = 0.4
OPT_AT1 = True
OPT_HT2X2 = True
OPT_HSQ_DVE = True

QC0 = 0
KC0 = 512
UC0 = 768
VC0 = 1280


class _Op:
    __slots__ = ("eng", "fn", "deps", "lane", "idx", "sig", "waits", "is_dma", "cost", "nbytes", "tag", "est", "prio", "why")

    def __init__(self, eng, fn, lane):
        self.eng = eng
        self.fn = fn
        self.lane = lane
        self.is_dma = lane is not None
        self.deps = set()
        self.sig = None
        self.waits = []


class Prog:
    ENGS = ("sp", "act", "pool", "dve", "pe")

    def __init__(self):
        self.ops = []
        self.last_writer = {}
        self.readers = {}
        self.lane_counts = {}
        self.lane_waitall = set()
        self.cur_tag = None

    PER = {"act": (130.0, 1.25), "dve": (200.0, 1.3), "pool": (150.0, 2.5), "pe": (6.0, 0.42), "sp": (100.0, 0.0)}

    def op(self, eng, fn, reads=(), writes=(), lane=None, waitall=False, n=0, k=1, nbytes=0):
        if DBG_OPLIMIT is not None and len(self.ops) >= DBG_OPLIMIT:
            return None
        if DBG_PRINT:
            print("OP", len(self.ops), eng, "R", list(reads)[:3], "W", list(writes)[:3], flush=True)
        writes = list(writes) + [r for r in reads if r[0] == "ps" and r not in writes]
        reads = [r for r in reads if r[0] != "ps"]
        o = _Op(eng, fn, lane)
        o.idx = len(self.ops)
        fx, pe_ = self.PER[eng]
        o.cost = k * fx + n * pe_
        o.nbytes = nbytes
        o.tag = self.cur_tag
        if self.cur_tag is None:
            o.prio = -10
        elif OPT_TILEPRIO:
            if self.cur_tag[1] == 'F' and OPT_FTOP:
                o.prio = -5
            else:
                o.prio = (self.cur_tag[0] + 0.5) if self.cur_tag[1] in OPT_PRIO1 else (self.cur_tag[0] - (OPT_MIXLEAD if self.cur_tag[0] >= OPT_LEADFROM else OPT_MIXLEAD0))
        else:
            o.prio = 1 if self.cur_tag[1] in OPT_PRIO1 else 0
        o.est = 0.0
        for r in reads:
            w = self.last_writer.get(r)
            if w is not None:
                o.deps.add(w)
        for r in writes:
            w = self.last_writer.get(r)
            if w is not None:
                o.deps.add(w)
            for rd in self.readers.get(r, ()):
                o.deps.add(rd)
        for r in writes:
            self.last_writer[r] = o.idx
            self.readers[r] = []
        for r in reads:
            if r not in writes:
                self.readers.setdefault(r, []).append(o.idx)
        o.deps.discard(o.idx)
        if lane is not None and waitall:
            o.deps = {d for d in o.deps if self.ops[d].lane != lane}
        if lane is not None:
            self.lane_counts[lane] = self.lane_counts.get(lane, 0) + 1
            o.sig = ("lane:" + lane, 16 * self.lane_counts[lane])
            if waitall:
                self.lane_waitall.add(lane)
        self.ops.append(o)
        return o

    def schedule(self):
        import heapq
        ops = self.ops
        n = len(ops)
        succ = [[] for _ in range(n)]
        indeg = [0] * n
        for o in ops:
            for d in o.deps:
                succ[d].append(o.idx)
                indeg[o.idx] += 1
        ready = [0.0] * n
        rdep = [None] * n
        lastop = {e: None for e in self.ENGS}
        rdep = [None] * n
        lastop = {e: None for e in self.ENGS}
        pending = {e: [] for e in self.ENGS}
        avail = {e: [] for e in self.ENGS}
        for o in ops:
            if indeg[o.idx] == 0:
                heapq.heappush(pending[o.eng], (0.0, o.idx))
        free = {e: 0.0 for e in self.ENGS}
        dma_free = 0.0
        order = []
        while len(order) < n:
            best = None
            for e in self.ENGS:
                pe_, av = pending[e], avail[e]
                while pe_ and pe_[0][0] <= free[e]:
                    i_ = heapq.heappop(pe_)[1]
                    heapq.heappush(av, (ops[i_].prio, i_))
                if av:
                    cand = (free[e], av[0][1], e, True)
                elif pe_:
                    cand = (pe_[0][0], pe_[0][1], e, False)
                else:
                    continue
                if best is None or cand[:2] < best[:2]:
                    best = cand
            start, idx, e, from_av = best
            if from_av:
                heapq.heappop(avail[e])
            else:
                heapq.heappop(pending[e])
            o = ops[idx]
            if o.is_dma:
                free[e] = start + (1000.0 if e == "pool" else 100.0)
                d0 = max(start, dma_free)
                dma_free = d0 + o.nbytes / 140.0
                fin = dma_free + 1800.0
            else:
                free[e] = start + o.cost
                fin = free[e]
            order.append(idx)
            o.est = start
            o.why = ('dep', rdep[idx]) if (ready[idx] >= start - 1e-6 and rdep[idx] is not None) else ('eng', lastop[e])
            lastop[e] = idx
            o.why = ('dep', rdep[idx]) if (ready[idx] >= start - 1e-6 and rdep[idx] is not None) else ('eng', lastop[e])
            lastop[e] = idx
            for sidx in succ[idx]:
                so = ops[sidx]
                lat = 80.0 if (so.eng == e and not o.is_dma) else OPT_LAT
                if fin + lat > ready[sidx]:
                    ready[sidx] = fin + lat
                    rdep[sidx] = idx
                    rdep[sidx] = idx
                indeg[sidx] -= 1
                if indeg[sidx] == 0:
                    heapq.heappush(pending[so.eng], (ready[sidx], sidx))
        self.est_total = max(free.values())
        self.sched_old_ops = ops
        self.sched_old_ops = ops
        newidx = {old: new for new, old in enumerate(order)}
        new_ops = [ops[i] for i in order]
        for o in new_ops:
            o.deps = {newidx[d] for d in o.deps}
            o.idx = newidx[o.idx]
        self.ops = new_ops

    def finalize(self):
        ops = self.ops
        needed = set()
        for o in ops:
            for d in o.deps:
                do = ops[d]
                if do.is_dma:
                    continue
                if do.eng == "pe" and o.eng == "pe" and not o.is_dma:
                    continue
                needed.add(d)
        cnt = {e: 0 for e in self.ENGS}
        for o in ops:
            if o.is_dma:
                continue
            if o.idx in needed:
                cnt[o.eng] += 1
                o.sig = ("eng:" + o.eng, cnt[o.eng])
        seen = {e: {} for e in self.ENGS}
        for o in ops:
            req = {}
            for d in o.deps:
                do = ops[d]
                if (not do.is_dma) and do.eng == "pe" and o.eng == "pe" and not o.is_dma:
                    continue
                key, val = do.sig
                if do.is_dma and do.lane in self.lane_waitall:
                    val = 16 * self.lane_counts[do.lane]
                if val > req.get(key, 0):
                    req[key] = val
            s = seen[o.eng]
            for key, val in req.items():
                if s.get(key, 0) >= val:
                    continue
                s[key] = val
                o.waits.append((key, val))

    def emit(self, nc, sems, block, final_waits):
        per = {e: [o for o in self.ops if o.eng == e] for e in self.ENGS}

        def run(e, name):
            for o in per[name]:
                for key, val in o.waits:
                    e.wait_ge(sems[key], val)
                ins = o.fn(e)
                if o.sig is not None:
                    ins.then_inc(sems[o.sig[0]], 16 if o.is_dma else 1)
            if name == "sp":
                for key, val in final_waits:
                    e.wait_ge(sems[key], val)

        @block.sync
        def _(e):
            run(e, "sp")

        @block.scalar
        def _(e):
            run(e, "act")

        @block.gpsimd
        def _(e):
            run(e, "pool")

        @block.vector
        def _(e):
            run(e, "dve")

        @block.tensor
        def _(e):
            run(e, "pe")


def build_nc():
    nc = bass.Bass("TRN2", target_bir_lowering=False)
    P = Prog()

    def din(name, shape, dt=F32):
        return nc.dram_tensor(name, list(shape), dt, kind="ExternalInput").ap()

    xin = din("x", [TOK, D])
    w_in_p = din("w_in_p", [D, NCOL])
    w_out_d = din("w_out", [D, D])
    w_up_d = din("w_up", [D, DFF])
    w_dn_d = din("w_down", [DFF, D])
    w_pool_d = din("w_pool", [512, 128])
    g1_d = din("g1", [1, D])
    g2_d = din("g2", [1, D])
    gf_d = din("gf", [1, D])
    psc_d = din("pscale", [128, 4])
    sink_d = din("sinks", [1, 8])
    cos_d = din("cosT", [128, SEQ])
    sin_d = din("sinT", [128, SEQ])
    rot_d = din("rotm", [128, 128])
    idn_d = din("ident", [128, 128])
    msk_d = din("maskcn", [128, 256])
    icn_d = din("invcnt", [1, 64])
    yout = nc.dram_tensor("y", [TOK, D], F32, kind="ExternalOutput").ap()
    wup_s = nc.dram_tensor("wup_s", [D, DFF], BF16, kind="Internal").ap()
    wdn_s = nc.dram_tensor("wdn_s", [DFF, D], BF16, kind="Internal").ap()

    import contextlib
    es_ = contextlib.ExitStack()
    with es_:
        def sb(name, shape, dt):
            return es_.enter_context(nc.sbuf_tensor(name, list(shape), dt))

        LU = 16 + T
        wcat = sb("wcat", [128, 8, NCOL], BF16)
        wout = sb("wout", [128, 8, D], BF16)
        wpool = sb("wpool", [128, 4, 128], BF16)
        wu = [sb(f"wu{i}", [128, 8, 512], BF16) for i in range(NSLOT)]
        wd = [sb(f"wd{i}", [128, 4, D], BF16) for i in range(NSLOT)]
        xs = [sb(f"xres{i}", [128, NB, D], F32) for i in range(2)]
        hT1 = sb("hT1", [128, 8, T], BF16)
        hT2s = [sb(f"hT2_{i}", [128, 8, T], BF16) for i in range(2 if OPT_HT2X2 else 1)]
        junkH = sb("junkH", [128, D], BF16) if OPT_JUNKH else None
        hbuf = [sb(f"hbuf{i}", [128, D], BF16) for i in range(2)]
        qT = sb("qT", [128, 4, T], BF16)
        kT = [sb(f"kT{i}", [128, 2, T], BF16) for i in range(2)]
        vaug = [sb(f"vaug{i}", [128, NB, 2, 128], BF16) for i in range(2)]
        ub1 = sb("ub", [128, LU], F32)
        ub = [ub1, ub1]
        xA = sb("xA", [128, D], F32)
        hsave = sb("hsave", [128, 4, 16], F32)
        tmpA = sb("tmpA", [128, LU], F32)
        tmpB = sb("tmpB", [128, LU], F32)
        dT = [sb(f"dT{i}", [128, T], BF16) for i in range(2)]
        yT = sb("yT", [128, 4, T], BF16)
        attnT = sb("attnT", [128, 4, T], BF16)
        PT = [[sb(f"PT{i}{k}", [128, 4, 256], BF16) for k in range(2)] for i in range(2)]
        if OPT_AT1:
            aT0_ = sb("aT0", [128, 8, T], BF16)
            aT = [aT0_, aT0_]
        else:
            aT = [sb(f"aT{i}", [128, 8, T], BF16) for i in range(2)]
        cosb = sb("cosb", [128, T], F32)
        sinb = sb("sinb", [128, T], F32)
        g1b = sb("g1b", [128, D], BF16)
        g2b = sb("g2b", [128, D], BF16)
        gfb = sb("gfb", [128, D], BF16)
        maskt = sb("maskt", [128, 256], BF16)
        rotm = sb("rotm_s", [128, 128], BF16)
        ident = sb("ident_s", [128, 128], BF16)
        sinkt = sb("sinkt", [128, 8], F32)
        est = sb("est", [128, 8], F32)
        psct = sb("psct", [128, 4], F32)
        icnt = sb("icnt", [128, 64], F32)
        st = sb("stats", [128, 48], F32)
        fix = sb("fixt", [128, 16], F32)
        xb = dT[0]
        ropeA, ropeB = tmpA, tmpB
        sA, sB = tmpA, tmpB

        banks = [es_.enter_context(nc.psum_tensor(f"bank{i}", [128, 512], F32)) for i in range(8)]
        UB = (0, 1)
        DB = (2, 3)
        MB = (4, 5, 6, 7)

        PL = "prolog"
        PW = "prolog_w"
        for kc in range(8):
            P.op("pool", lambda e, kc=kc: e.dma_start(out=wcat[:, kc, :], in_=w_in_p[kc * 128:(kc + 1) * 128, :]),
                 writes=[("wcat",)], lane=PL, waitall=True, nbytes=128 * NCOL * 4)
        for kc in range(8):
            P.op("pool", lambda e, kc=kc: e.dma_start(out=wout[:, kc, :], in_=w_out_d[kc * 128:(kc + 1) * 128, :]),
                 writes=[("wout",)], lane=PL, waitall=True, nbytes=128 * D * 4)
        for g in range(4):
            P.op("pool", lambda e, g=g: e.dma_start(out=wpool[:, g, :], in_=w_pool_d[g * 128:(g + 1) * 128, :]),
                 writes=[("wpool",)], lane=PL, waitall=True, nbytes=65536)
        P.op("pool", lambda e: e.dma_start(out=rotm[:], in_=rot_d[:, :]), writes=[("consts",)], lane=PL, waitall=True, nbytes=65536)
        P.op("pool", lambda e: e.dma_start(out=ident[:], in_=idn_d[:, :]), writes=[("consts",)], lane=PL, waitall=True, nbytes=65536)
        P.op("pool", lambda e: e.dma_start(out=maskt[:], in_=msk_d[:, :]), writes=[("consts",)], lane=PL, waitall=True, nbytes=131072)
        P.op("pool", lambda e: e.dma_start(out=g1b[:], in_=g1_d.partition_broadcast(128)), writes=[("g1b",)], lane=PL, waitall=True, nbytes=4096)
        P.op("pool", lambda e: e.dma_start(out=g2b[:], in_=g2_d.partition_broadcast(128)), writes=[("g2b",)], lane=PL, waitall=True, nbytes=4096)
        P.op("pool", lambda e: e.dma_start(out=gfb[:], in_=gf_d.partition_broadcast(128)), writes=[("gfb",)], lane=PL, waitall=True, nbytes=4096)
        for i in range(8):
            P.op("pool", lambda e, i=i: e.dma_start(out=wup_s[i * 128:(i + 1) * 128, :], in_=w_up_d[i * 128:(i + 1) * 128, :]),
                 writes=[("wup_s",)], lane=PW, waitall=True, nbytes=128 * DFF * 4)
        for i in range(8):
            P.op("pool", lambda e, i=i: e.dma_start(out=wdn_s[i * 512:(i + 1) * 512, :], in_=w_dn_d[i * 512:(i + 1) * 512, :]),
                 writes=[("wdn_s",)], lane=PW, waitall=True, nbytes=512 * D * 4)
        PS = "prolog_sp"
        P.op("sp", lambda e: e.dma_start(out=sinkt[:], in_=sink_d.partition_broadcast(128)), writes=[("sinkt",)], lane=PS, waitall=True, nbytes=4096)
        P.op("sp", lambda e: e.dma_start(out=psct[:], in_=psc_d[:, :]), writes=[("psct",)], lane=PS, waitall=True, nbytes=2048)
        P.op("sp", lambda e: e.dma_start(out=icnt[:], in_=icn_d.partition_broadcast(128)), writes=[("icnt",)], lane=PS, waitall=True, nbytes=32768)
        P.op("act", lambda e: e.activation(out=est[:], in_=sinkt[:], func=AF.Exp), reads=[("sinkt",)], writes=[("est",)], n=8)
        for i in range(2):
            P.op("pool", lambda e, i=i: e.memset(vaug[i][:], 1.0), writes=[("vaug", i, b) for b in range(NB)], n=1024)

        def rstd_ops(xt, xk, b, junk, junk_key, sc, src=None):
            c_ss, c_ln, c_rs = sc + b, sc + 4 + b, sc + 8 + b
            if src is None:
                src = xt[:, b, :]
            P.op("act", lambda e: e.activation(out=junk, in_=src, func=AF.Square, accum_out=st[:, c_ss:c_ss + 1]),
                 reads=[(xk, b)], writes=(junk_key if isinstance(junk_key, list) else [junk_key]) + [("st", c_ss)], n=D)
            P.op("act", lambda e: e.activation(out=st[:, c_ln:c_ln + 1], in_=st[:, c_ss:c_ss + 1], func=AF.Ln,
                                               scale=1.0 / D, bias=EPS),
                 reads=[("st", c_ss)], writes=[("st", c_ln)], n=1)
            P.op("act", lambda e: e.activation(out=st[:, c_rs:c_rs + 1], in_=st[:, c_ln:c_ln + 1], func=AF.Exp, scale=-0.5),
                 reads=[("st", c_ln)], writes=[("st", c_rs)], n=1)
            return c_rs

        def norm_transpose_phase(xt, xk, gb, gkey, sc, tbanks, hT, hk, stage_rows=None, hbufs=None, hkeys=None):
            for b in range(NB):
                if hbufs is None:
                    hb, hbk = hbuf[b % 2], [("hbuf", b % 2)]
                else:
                    hb, hbk = hbufs[b % 2], hkeys[b % 2]
                if stage_rows is not None:
                    rr = stage_rows + b * 128
                    P.op("sp", lambda e, rr=rr: e.dma_start(out=xA[:], in_=xin[rr:rr + 128, :]),
                         writes=[("xA", bb) for bb in range(NB)], lane="xa", nbytes=1 << 19)
                    xsrc = xA[:]
                    c_rs = rstd_ops(None, xk, b, hb[:], hbk, sc, src=xsrc)
                else:
                    xsrc = xt[:, b, :]
                    c_rs = rstd_ops(xt, xk, b, hb[:], hbk, sc)
                P.op("dve", lambda e, xsrc=xsrc, hb=hb, c_rs=c_rs: e.scalar_tensor_tensor(
                    out=hb[:], in0=xsrc, scalar=st[:, c_rs:c_rs + 1], in1=gb[:], op0=ALU.mult, op1=ALU.mult),
                    reads=[(xk, b), ("st", c_rs), gkey], writes=hbk, n=D)
                bk = tbanks[b % 2]
                pst = banks[bk][:].bitcast(BF16)

                def tr(e, hb=hb, pst=pst):
                    ins = None
                    for kc in range(8):
                        ins = e.transpose(out=pst[:, kc * 128:(kc + 1) * 128], in_=hb[:, kc * 128:(kc + 1) * 128], identity=ident[:])
                    return ins
                P.op("pe", tr, reads=hbk + [("consts",)], writes=[("ps", bk)], n=1024, k=8)
                P.op("act", lambda e, b=b, pst=pst: e.activation(
                    out=hT[:, :, b * 128:(b + 1) * 128], in_=pst.rearrange("p (k t) -> p k t", k=8), func=AF.Copy),
                    reads=[("ps", bk)], writes=[(hk, b)], n=1024)

        def mm_group(out_ap, pairs, reads, bk):
            def f(e):
                ins = None
                n = len(pairs)
                for i, (l, r, _) in enumerate(pairs):
                    ins = e.matmul(out_ap, lhsT=l, rhs=r, start=(i == 0), stop=(i == n - 1))
                return ins
            P.op("pe", f, reads=reads, writes=[("ps", bk)], n=sum(p[2] for p in pairs), k=len(pairs))

        H1_ALL = [("hT1", b) for b in range(NB)]
        NPT = NT * NPIECE

        def load_wu(gp):
            slot, pc = gp % NSLOT, gp % NPIECE
            P.op("sp", lambda e: e.dma_start(
                out=wu[slot][:], in_=wup_s.rearrange("(kc p) f -> p kc f", p=128)[:, :, pc * 512:(pc + 1) * 512]),
                reads=[("wup_s",)], writes=[("wu", slot)], lane=f"wu{slot}", nbytes=1 << 20)

        def load_wd(gp):
            slot, pc = gp % NSLOT, gp % NPIECE
            P.op("sp", lambda e: e.dma_start(
                out=wd[slot][:], in_=wdn_s[pc * 512:(pc + 1) * 512, :].rearrange("(fc p) d -> p fc d", p=128)),
                reads=[("wdn_s",)], writes=[("wd", slot)], lane=f"wd{slot}", nbytes=1 << 20)

        def mixer(ti):
            tpos = ti % TPS
            par = ti % 2
            r0 = ti * T
            t0 = tpos * T
            x = xs[par]
            xk = "x%d" % par
            P.cur_tag = (ti, 'L')
            for b in range(NB):
                P.op("sp", lambda e, b=b: e.dma_start(out=x[:, b, :], in_=xin[r0 + b * 128: r0 + (b + 1) * 128, :]),
                     writes=[(xk, b)], lane=f"xld{par}{b}", nbytes=1 << 19)
            P.op("sp", lambda e: e.dma_start(out=cosb[:], in_=cos_d[:, t0:t0 + T]), writes=[("cos",)], lane="cos", nbytes=1 << 18)
            P.op("sp", lambda e: e.dma_start(out=sinb[:], in_=sin_d[:, t0:t0 + T]), writes=[("sin",)], lane="sin", nbytes=1 << 18)
            if ti == 0:
                for gp in range(NSLOT):
                    load_wu(gp)
                    load_wd(gp)
            if DBG_STOP == 'prolog':
                return False
            P.cur_tag = (ti, 'A')
            if OPT_XA and ti >= OPT_XAFROM:
                norm_transpose_phase(None, "xA", g1b, ("g1b",), 0, (MB[0], MB[1]), hT1, "hT1", stage_rows=r0)
            else:
                norm_transpose_phase(x, xk, g1b, ("g1b",), 0, (MB[0], MB[1]), hT1, "hT1")
            if DBG_STOP == 'A':
                return False

            P.cur_tag = (ti, 'B')
            pbank = (MB[0], MB[1])
            rbank = (MB[2], MB[3])
            chunk_list = [("q", c) for c in range(4)] + [("k", kv) for kv in range(2)]
            for ci, (kind, j) in enumerate(chunk_list):
                col0 = (QC0 if kind == "q" else KC0) + j * 128
                bk = pbank[ci % 2]
                rb = rbank[ci % 2]
                pairs = [(wcat[:, kc, col0:col0 + 128], hT1[:, kc, :], T) for kc in range(8)]
                mm_group(banks[bk][:], pairs, [("wcat",)] + H1_ALL, bk)
                P.op("act", lambda e, bk=bk: e.activation(out=xb[:], in_=banks[bk][:], func=AF.Copy),
                     reads=[("ps", bk)], writes=[("dT", 0)], n=T)
                mm_group(banks[rb][:], [(rotm[:], xb[:], T)], [("dT", 0), ("consts",)], rb)
                P.op("dve", lambda e, bk=bk: e.tensor_tensor(out=ropeA[:, 0:T], in0=banks[bk][:], in1=cosb[:], op=ALU.mult),
                     reads=[("ps", bk), ("cos",)], writes=[("tmpA",)], n=T)
                P.op("dve", lambda e, rb=rb: e.tensor_tensor(out=ropeB[:, 0:T], in0=banks[rb][:], in1=sinb[:], op=ALU.mult),
                     reads=[("ps", rb), ("sin",)], writes=[("tmpB",)], n=T)
                if kind == "q":
                    dst, wkey = qT[:, j, :], [("qT", j)]
                else:
                    dst, wkey = kT[par][:, j, :], [("kT", par, j)]
                P.op("pool", lambda e, dst=dst: e.tensor_tensor(out=dst, in0=ropeA[:, 0:T], in1=ropeB[:, 0:T], op=ALU.add),
                     reads=[("tmpA",), ("tmpB",)], writes=wkey, n=T)
            vb = MB[0]

            def vmm(e):
                ins = None
                for b in range(NB):
                    for kc in range(8):
                        ins = e.matmul(banks[vb][:, b * 128:(b + 1) * 128], lhsT=hT1[:, kc, b * 128:(b + 1) * 128],
                                       rhs=wcat[:, kc, VC0:VC0 + 128], start=(kc == 0), stop=(kc == 7))
                return ins
            P.op("pe", vmm, reads=[("wcat",)] + H1_ALL, writes=[("ps", vb)], n=32 * 128, k=32)
            P.op("act", lambda e: e.activation(
                out=vaug[par][:, :, :, 0:64], in_=banks[vb][:].rearrange("p (b k d) -> p b k d", b=NB, k=2), func=AF.Copy),
                reads=[("ps", vb)], writes=[("vaug", par, b) for b in range(NB)], n=T)
            if DBG_STOP == 'B':
                return False

            P.cur_tag = (ti, 'C')
            for g in range(4):
                w = POOL_W[g]
                L = LU
                uu = ub[g % 2]
                ukey = ("ub", 0)
                bk = MB[g % 2]
                yb = MB[2 + (g % 2)]
                col0 = UC0 + g * 128
                pairs = [(wcat[:, kc, col0:col0 + 128], hT1[:, kc, :], T) for kc in range(8)]
                mm_group(banks[bk][:], pairs, [("wcat",)] + H1_ALL, bk)
                if tpos == 0:
                    P.op("pool", lambda e, uu=uu: e.memset(uu[:, 0:16], 0.0), writes=[ukey], n=16)
                else:
                    P.op("pool", lambda e, uu=uu, g=g: e.tensor_copy(out=uu[:, 0:16], in_=hsave[:, g, :]),
                         reads=[("hsave", g)], writes=[ukey], n=16)
                P.op("act", lambda e, bk=bk, uu=uu, w=w: e.activation(out=uu[:, 16:L], in_=banks[bk][:], func=AF.Copy, scale=1.0 / w),
                     reads=[("ps", bk)], writes=[ukey], n=T)
                P.op("pool", lambda e, uu=uu, g=g: e.tensor_copy(out=hsave[:, g, :], in_=uu[:, T:T + 16]),
                     reads=[ukey], writes=[("hsave", g)], n=16)
                P.op("pool", lambda e, uu=uu: e.tensor_tensor(out=sA[:, 1:L], in0=uu[:, 1:L], in1=uu[:, 0:L - 1], op=ALU.add),
                     reads=[ukey], writes=[("tmpA",)], n=L)
                S_t, S_key = sA, ("tmpA",)
                if w >= 4:
                    P.op("pool", lambda e: e.tensor_tensor(out=sB[:, 3:L], in0=sA[:, 3:L], in1=sA[:, 1:L - 2], op=ALU.add),
                         reads=[("tmpA",)], writes=[("tmpB",)], n=L)
                    S_t, S_key = sB, ("tmpB",)
                if w >= 8:
                    P.op("pool", lambda e: e.tensor_tensor(out=sA[:, 7:L], in0=sB[:, 7:L], in1=sB[:, 3:L - 4], op=ALU.add),
                         reads=[("tmpB",)], writes=[("tmpA",)], n=L)
                    S_t, S_key = sA, ("tmpA",)
                if w >= 16:
                    P.op("pool", lambda e: e.tensor_tensor(out=sB[:, 15:L], in0=sA[:, 15:L], in1=sA[:, 7:L - 8], op=ALU.add),
                         reads=[("tmpA",)], writes=[("tmpB",)], n=L)
                    S_t, S_key = sB, ("tmpB",)
                dd = dT[g % 2]
                P.op("dve", lambda e, S_t=S_t, dd=dd, bk=bk: e.tensor_tensor(
                    out=dd[:], in0=S_t[:, 16:L], in1=banks[bk][:], op=ALU.subtract),
                    reads=[S_key, ("ps", bk)], writes=[("dT", g % 2)], n=T)
                if tpos == 0:
                    P.op("pool", lambda e, g=g, S_t=S_t: e.tensor_tensor(
                        out=fix[:], in0=S_t[:, 16:32], in1=icnt[:, g * 16:(g + 1) * 16], op=ALU.mult),
                        reads=[S_key, ("icnt",)], writes=[("fix",)], n=16)
                    P.op("dve", lambda e, dd=dd, bk=bk: e.tensor_tensor(out=dd[:, 0:16], in0=fix[:], in1=banks[bk][:, 0:16], op=ALU.subtract),
                         reads=[("fix",), ("ps", bk)], writes=[("dT", g % 2)], n=16)
                mm_group(banks[yb][:], [(wpool[:, g, :], dd[:], T)], [("wpool",), ("dT", g % 2)], yb)
                P.op("act", lambda e, g=g, yb=yb: e.activation(out=yT[:, g, :], in_=banks[yb][:], func=AF.Copy, scale=psct[:, g:g + 1]),
                     reads=[("ps", yb), ("psct",)], writes=[("yT", g)], n=T)
            if DBG_STOP == 'C':
                return False

            P.cur_tag = (ti, 'D')
            dent, denr = tmpA, tmpB
            groups = ([-1] if tpos > 0 else []) + [0, 1, 2, 3]
            for kb in groups:
                gpar = kb % 2
                if kb == -1:
                    ksrc, kcol, kpar = kT[1 - par], 3 * 128, 1 - par
                    q0, nq, pcol = 0, 128, 128
                else:
                    ksrc, kcol, kpar = kT[par], kb * 128, par
                    q0 = kb * 128
                    nq = 256 if kb < 3 else 128
                    pcol = 0
                for kv in range(2):
                    ptile = PT[gpar][kv]
                    bX, bY = MB[0], MB[1]

                    def smm(e, ksrc=ksrc, kcol=kcol, kv=kv, q0=q0, nq=nq):
                        ins = None
                        for hp in range(2):
                            c = kv * 2 + hp
                            for half in range(2):
                                bk = bX if half == 0 else bY
                                ins = e.matmul(banks[bk][:, hp * 256: hp * 256 + nq],
                                               lhsT=ksrc[half * 64:(half + 1) * 64, kv, kcol:kcol + 128],
                                               rhs=qT[half * 64:(half + 1) * 64, c, q0:q0 + nq], start=True, stop=True)
                        return ins
                    P.op("pe", smm, reads=[("kT", kpar, kv), ("qT", kv * 2), ("qT", kv * 2 + 1)],
                         writes=[("ps", bX), ("ps", bY)], n=2 * nq, k=4)
                    for half, bk in ((0, bX), (1, bY)):
                        P.op("act", lambda e, bk=bk, ptile=ptile, half=half, nq=nq, pcol=pcol: e.activation(
                            out=ptile[:, half::2, pcol:pcol + nq],
                            in_=banks[bk][:].rearrange("p (h q) -> p h q", h=2)[:, :, 0:nq], func=AF.Exp, scale=0.125),
                            reads=[("ps", bk)], writes=[("PT", gpar, kv, half)], n=2 * nq)
                    P.op("pool", lambda e, ptile=ptile, nq=nq, pcol=pcol: e.tensor_tensor(
                        out=ptile[:, :, pcol:pcol + nq], in0=ptile[:, :, pcol:pcol + nq],
                        in1=maskt[:, pcol:pcol + nq].unsqueeze(1).broadcast_to([128, 4, nq]), op=ALU.mult),
                        reads=[("PT", gpar, kv, 0), ("PT", gpar, kv, 1), ("consts",)],
                        writes=[("PT", gpar, kv, 0), ("PT", gpar, kv, 1)], n=4 * nq)
                if kb < 0:
                    continue
                b = kb
                has_prev = not (tpos == 0 and kb == 0)
                for kv in range(2):
                    ob = MB[2 + kv]
                    pairs = []
                    rd = [("PT", gpar, kv, 0), ("PT", gpar, kv, 1), ("vaug", par, b)]
                    if has_prev:
                        if kb == 0:
                            pv, pblk, ppar = vaug[1 - par], 3, 1 - par
                        else:
                            pv, pblk, ppar = vaug[par], kb - 1, par
                        pairs.append((pv[:, pblk, kv, :], PT[1 - gpar][kv][:, :, 128:256], 512))
                        rd += [("PT", 1 - gpar, kv, 0), ("PT", 1 - gpar, kv, 1), ("vaug", ppar, pblk)]
                    pairs.append((vaug[par][:, b, kv, :], PT[gpar][kv][:, :, 0:128], 512))
                    O3 = banks[ob][:].rearrange("p (h q) -> p h q", h=4)
                    mm_group(O3, pairs, rd, ob)
                    P.op("dve", lambda e, O3=O3, kv=kv: e.tensor_tensor(
                        out=dent[64:128, 0:512].rearrange("p (h q) -> p h q", h=4), in0=O3[64:128, :, :],
                        in1=est[64:128, kv * 4:(kv + 1) * 4].unsqueeze(2).broadcast_to([64, 4, 128]), op=ALU.add),
                        reads=[("ps", ob), ("est",)], writes=[("tmpA",)], n=512)
                    P.op("act", lambda e: e.activation(out=dent[64:128, 0:512], in_=dent[64:128, 0:512], func=AF.Ln),
                         reads=[("tmpA",)], writes=[("tmpA",)], n=512)
                    P.op("act", lambda e: e.activation(out=denr[0:64, 0:512], in_=dent[64:128, 0:512], func=AF.Exp, scale=-1.0),
                         reads=[("tmpA",)], writes=[("tmpB",)], n=512)
                    R3 = denr[:, 0:512].rearrange("p (h q) -> p h q", h=4)
                    for odd in range(2):
                        P.op("dve", lambda e, O3=O3, odd=odd, kv=kv, b=b, R3=R3: e.tensor_tensor(
                            out=attnT[odd * 64:(odd + 1) * 64, kv * 2:kv * 2 + 2, b * 128:(b + 1) * 128],
                            in0=O3[0:64, odd::2, :], in1=R3[0:64, odd::2, :], op=ALU.mult),
                            reads=[("ps", ob), ("tmpB",)], writes=[("attnT", kv, b, odd)], n=256)
            if DBG_STOP == 'D':
                return False

            P.cur_tag = (ti, 'E')
            oi = 0
            for b in range(NB):
                for half in range(2):
                    bk = MB[oi % 4]
                    oi += 1
                    pairs = []
                    for c in range(4):
                        pairs.append((attnT[:, c, b * 128:(b + 1) * 128], wout[:, c, half * 512:(half + 1) * 512], 512))
                    for g in range(4):
                        pairs.append((yT[:, g, b * 128:(b + 1) * 128], wout[:, 4 + g, half * 512:(half + 1) * 512], 512))
                    rd = [("wout",)] + [("attnT", kv, b, odd) for kv in range(2) for odd in range(2)] + [("yT", g) for g in range(4)]
                    mm_group(banks[bk][:], pairs, rd, bk)
                    P.op("dve", lambda e, b=b, half=half, bk=bk: e.tensor_tensor(
                        out=x[:, b, half * 512:(half + 1) * 512], in0=banks[bk][:], in1=x[:, b, half * 512:(half + 1) * 512], op=ALU.add),
                        reads=[("ps", bk), (xk, b)], writes=[(xk, b)], n=512)
            if DBG_STOP == 'E':
                return False
            return True

        def mlp(ti):
            par = ti % 2
            hT2 = hT2s[par % len(hT2s)]
            h2k = "hT2_%d" % (par % len(hT2s))
            H2_ALL = [(h2k, b) for b in range(NB)]
            r0 = ti * T
            x = xs[par]
            xk = "x%d" % par
            P.cur_tag = (ti, 'F')
            yflat = yT[:].rearrange("p g t -> p (g t)")
            norm_transpose_phase(x, xk, g2b, ("g2b",), 12, (DB[0], DB[1]) if OPT_FBANKS == 'DB' else (MB[2], MB[3]), hT2, h2k,
                                 hbufs=[yflat[:, 0:D], yflat[:, D:2 * D]] if OPT_FY else None,
                                 hkeys=[[("yT", 0), ("yT", 1)], [("yT", 2), ("yT", 3)]] if OPT_FY else None)
            if DBG_STOP == 'F':
                return False
            P.cur_tag = (ti, 'G')
            ui = 0
            di = 0
            for ga in range(NPIECE // 2):
                abuf = aT[ga % 2]
                slots = []
                gp0 = ti * NPIECE + ga * 2
                for pp in range(2):
                    gp = gp0 + pp
                    slot = gp % NSLOT
                    slots.append(slot)
                    for f in range(4):
                        fi = pp * 4 + f
                        bk = UB[ui % 2]
                        ui += 1
                        pairs = [(wu[slot][:, kc, f * 128:(f + 1) * 128], hT2[:, kc, :], T) for kc in range(8)]
                        mm_group(banks[bk][:], pairs, [("wu", slot)] + H2_ALL, bk)
                        P.op("act", lambda e, bk=bk: e.activation(out=banks[bk][:], in_=banks[bk][:], func=AF.Relu),
                             reads=[("ps", bk)], writes=[("ps", bk)], n=T)
                        P.op("act", lambda e, bk=bk, abuf=abuf, fi=fi: e.activation(out=abuf[:, fi, :], in_=banks[bk][:], func=AF.Square),
                             reads=[("ps", bk)], writes=[("aT", (0 if OPT_AT1 else ga % 2), fi)], n=T)
                    if gp + NSLOT < NPT:
                        load_wu(gp + NSLOT)
                for b in range(NB):
                    for half in range(2):
                        bk = DB[di % 2]
                        di += 1
                        pairs = []
                        for fi in range(8):
                            pairs.append((abuf[:, fi, b * 128:(b + 1) * 128], wd[slots[fi // 4]][:, fi % 4, half * 512:(half + 1) * 512], 512))
                        rd = [("wd", s_) for s_ in slots] + [("aT", (0 if OPT_AT1 else ga % 2), fi) for fi in range(8)]
                        mm_group(banks[bk][:], pairs, rd, bk)
                        P.op("dve", lambda e, b=b, half=half, bk=bk: e.tensor_tensor(
                            out=x[:, b, half * 512:(half + 1) * 512], in0=banks[bk][:], in1=x[:, b, half * 512:(half + 1) * 512], op=ALU.add),
                            reads=[("ps", bk), (xk, b)], writes=[(xk, b)], n=512)
                for gq in (gp0 + NSLOT, gp0 + NSLOT + 1):
                    if gq < NPT:
                        load_wd(gq)
            if DBG_STOP == 'G':
                return False
            P.cur_tag = (ti, 'H')
            for b in range(NB):
                if OPT_JUNKH:
                    junk, jkey = junkH[:], ("junkH",)
                else:
                    junk, jkey = aT[0][:, 0:2, :], ("aT", 0, 0)
                if OPT_HSQ_DVE:
                    c_ss, c_ln, c_rs = 24 + b, 28 + b, 32 + b
                    P.op("dve", lambda e, b=b, junk=junk, c_ss=c_ss: e.scalar_tensor_tensor(
                        out=junk, in0=x[:, b, :], scalar=1.0, in1=x[:, b, :], op0=ALU.mult, op1=ALU.mult, accum_out=st[:, c_ss:c_ss + 1]),
                        reads=[(xk, b)], writes=[jkey, ("st", c_ss)], n=D)
                    P.op("act", lambda e, c_ss=c_ss, c_ln=c_ln: e.activation(out=st[:, c_ln:c_ln + 1], in_=st[:, c_ss:c_ss + 1], func=AF.Ln,
                                                                         scale=1.0 / D, bias=EPS),
                         reads=[("st", c_ss)], writes=[("st", c_ln)], n=1)
                    P.op("act", lambda e, c_ln=c_ln, c_rs=c_rs: e.activation(out=st[:, c_rs:c_rs + 1], in_=st[:, c_ln:c_ln + 1], func=AF.Exp, scale=-0.5),
                         reads=[("st", c_ln)], writes=[("st", c_rs)], n=1)
                else:
                    c_rs = rstd_ops(x, xk, b, junk, jkey, 24)
                P.op("act", lambda e, b=b, c_rs=c_rs: e.activation(
                    out=x[:, b, :], in_=x[:, b, :], func=AF.Copy, scale=st[:, c_rs:c_rs + 1]),
                    reads=[(xk, b), ("st", c_rs)], writes=[(xk, b)], n=D)
                P.op("pool", lambda e, b=b: e.tensor_tensor(out=x[:, b, :], in0=x[:, b, :], in1=gfb[:], op=ALU.mult),
                     reads=[(xk, b), ("gfb",)], writes=[(xk, b)], n=D)
                P.op("sp", lambda e, b=b: e.dma_start(out=yout[r0 + b * 128: r0 + (b + 1) * 128, :], in_=x[:, b, :]),
                     reads=[(xk, b)], lane=f"st{par}{b}", nbytes=1 << 19)
            return True

        for ti in range(NT):
            if not mixer(ti):
                break
            if not mlp(ti):
                break

        if DO_SCHEDULE:
            P.schedule()
        P.finalize()
        final_waits = []
        for l, c in P.lane_counts.items():
            if l.startswith("st") or DBG_STOP is not None:
                final_waits.append(("lane:" + l, 16 * c))

        sem_names = ["eng:" + e for e in Prog.ENGS] + ["lane:" + l for l in P.lane_counts]
        sems = {}
        for i, n in enumerate(sem_names):
            sems[n] = es_.enter_context(nc.semaphore(f"s{i}"))
        block = es_.enter_context(nc.Block())
        P.emit(nc, sems, block, final_waits)
    return nc


def _consts():
    half = 32
    inv_freq = (np.float32(10000.0) ** (-np.arange(half, dtype=np.float32) / np.float32(half))).astype(np.float32)
    pos = np.arange(SEQ, dtype=np.float32)
    ang = (pos[:, None] * inv_freq[None, :]).astype(np.float32)
    cosv = np.cos(ang.astype(np.float64)).astype(np.float32)
    sinv = np.sin(ang.astype(np.float64)).astype(np.float32)
    fidx = (np.arange(128) % 64) % 32
    cosT = np.ascontiguousarray(cosv[:, fidx].T)
    sinT = np.ascontiguousarray(sinv[:, fidx].T)
    rot = np.zeros((128, 128), np.float32)
    for m in range(128):
        base = (m // 64) * 64
        d = m % 64
        if d < 32:
            rot[base + d + 32, m] = -1.0
        else:
            rot[base + d - 32, m] = 1.0
    ident = np.eye(128, dtype=np.float32)
    j = np.arange(128)[:, None]
    q = np.arange(128)[None, :]
    mask = np.concatenate([(j <= q), (j > q)], axis=1).astype(np.float32)
    t = np.arange(16)
    icn = np.concatenate([w / np.minimum(t + 1, w) for w in POOL_W]).astype(np.float32)[None, :]
    return cosT, sinT, rot, ident, mask, icn


_NC_CACHE = {}


def kernel(x, attn_norm_g, w_in, attn_sinks, w_pool, pool_scale, w_out,
           mlp_norm_g, w_up, w_down, final_norm_g):
    x = np.asarray(x, dtype=np.float32)
    w_in0 = np.asarray(w_in, dtype=np.float32)[0]
    cols = np.concatenate([
        np.arange(0, 512),
        np.arange(512, 576), np.arange(512, 576),
        np.arange(576, 640), np.arange(576, 640),
        np.arange(768, 1280),
        np.arange(640, 768),
    ])
    w_in_p = np.ascontiguousarray(w_in0[:, cols])
    cosT, sinT, rot, ident, mask, icn = _consts()
    shared = {
        "w_in_p": w_in_p,
        "w_out": np.ascontiguousarray(np.asarray(w_out, np.float32)[0]),
        "w_up": np.ascontiguousarray(np.asarray(w_up, np.float32)[0]),
        "w_down": np.ascontiguousarray(np.asarray(w_down, np.float32)[0]),
        "w_pool": np.ascontiguousarray(np.asarray(w_pool, np.float32)[0].reshape(512, 128)),
        "g1": np.ascontiguousarray(np.asarray(attn_norm_g, np.float32).reshape(1, D)),
        "g2": np.ascontiguousarray(np.asarray(mlp_norm_g, np.float32).reshape(1, D)),
        "gf": np.ascontiguousarray(np.asarray(final_norm_g, np.float32).reshape(1, D)),
        "pscale": np.ascontiguousarray(np.asarray(pool_scale, np.float32)[0].reshape(4, 128).T),
        "sinks": np.ascontiguousarray(np.asarray(attn_sinks, np.float32).reshape(1, 8)),
        "cosT": cosT, "sinT": sinT, "rotm": rot, "ident": ident, "maskcn": mask, "invcnt": icn,
    }
    in_maps = []
    for c in range(NCORES):
        m = dict(shared)
        m["x"] = np.ascontiguousarray(x[c * SEQ_PER_CORE:(c + 1) * SEQ_PER_CORE].reshape(TOK, D))
        in_maps.append(m)
    if "nc" not in _NC_CACHE:
        _NC_CACHE["nc"] = build_nc()
    nc = _NC_CACHE["nc"]
    res = run_bass_kernel_spmd(nc, in_maps, core_ids=list(range(NCORES)))
    out = np.concatenate([np.asarray(r["y"]).reshape(SEQ_PER_CORE, SEQ, D) for r in res.results], axis=0)
    return out.astype(np.float32, copy=False)
```
